# Optimizing a Trainium2 kernel written in Bass

```python
import jax, jax.numpy as jnp
from jax import lax
import numpy as np

D_MODEL = 1024
BATCH = 32
SEQ = 256
DEPTH = 2
DEC_BATCH = 4
DEC_SEQ = 2048
PAST_LEN = 512

GRID_W = 64
HA = 4
DKA = 128
DVA = 128
WA = HA * DVA
HB = 8
NB = 64
WB = HB * NB
R_DECAY = 64
R_AAA = 64
R_GATE = 128
RWKV_COLS = 3 * WB + 2 * R_DECAY + 2 * R_AAA + R_GATE
HC = 4
DHC = 128
WC = HC * DHC
D_FF = 2816
HGRN_CHUNK = 32
MLSTM_CHUNK = 64
RMS_EPS = 1e-6
RWKV_LN_EPS = 64e-5
IN_WIDTHS = (HA * DKA, HA * DKA, HA * DKA, WA, WA, RWKV_COLS, WC, WC, WC, WC, 4 * HC, 3 * D_MODEL)
D_IN = 3 * HA * DKA + 2 * WA + RWKV_COLS + 4 * WC + 4 * HC + 3 * D_MODEL
F32 = jnp.float32

kernel_name = 'hybrid_bidir_recurrent_diffusion_step'


def _split(y, widths):
    idx = np.cumsum(np.asarray(widths))[:-1].tolist()
    return jnp.split(y, idx, axis=-1)


def _rms(x, w, eps=RMS_EPS):
    x32 = x.astype(F32)
    y = x32 * lax.rsqrt(jnp.mean(x32 * x32, axis=-1, keepdims=True) + eps)
    return (y * w.astype(F32)).astype(x.dtype)


def _shift_pair(y, axis):
    n = y.shape[axis]
    yp = jnp.pad(y, [(1, 1) if a == axis else (0, 0) for a in range(y.ndim)])
    return lax.slice_in_dim(yp, 0, n, axis=axis), lax.slice_in_dim(yp, 2, n + 2, axis=axis)


def _token_shift(y, mu, grid):
    if grid:
        B, T, C = y.shape
        g = y.reshape(B, T // GRID_W, GRID_W, C)
        left, right = _shift_pair(g, 2)
        up, down = _shift_pair(g, 1)
        g = g + mu[0] * (left - g) + mu[1] * (right - g) + mu[2] * (up - g) + mu[3] * (down - g)
        return g.reshape(B, T, C)
    prev, nxt = _shift_pair(y, 1)
    return y + mu[0] * (prev - y) + mu[1] * (nxt - y)


def _gla_chunk_scan(q, k, v, logf, s0):
    B, T, H, K = q.shape
    V = v.shape[-1]
    L = HGRN_CHUNK
    n = T // L

    def chunks(a):
        return a.astype(F32).reshape(B, n, L, H, a.shape[-1]).transpose(1, 0, 3, 2, 4)

    causal = jnp.tril(jnp.ones((L, L), dtype=bool))[:, :, None]

    def step(S, inp):
        qc, kc, vc, gc = inp
        b = jnp.cumsum(gc, axis=2)
        rel = jnp.exp(jnp.where(causal, b[:, :, :, None, :] - b[:, :, None, :, :], -jnp.inf))
        A = jnp.einsum('bhtk,bhsk,bhtsk->bhts', qc, kc, rel)
        o = jnp.einsum('bhts,bhsv->bhtv', A, vc) + jnp.einsum('bhtk,bhkv->bhtv', qc * jnp.exp(b), S)
        b_end = b[:, :, -1:, :]
        S = jnp.exp(b_end[:, :, 0, :, None]) * S + jnp.einsum('bhsk,bhsv->bhkv', kc * jnp.exp(b_end - b), vc)
        return S, o

    S, o = lax.scan(step, s0.astype(F32), (chunks(q), chunks(k), chunks(v), chunks(logf)))
    return S, o.transpose(1, 0, 3, 2, 4).reshape(B, T, H, V)


def _hgrn2(hq, hf_fwd, hf_bwd, hi, hg, lb, norm_w, s0):
    B, T, _ = hq.shape
    q = hq.astype(F32).reshape(B, T, HA, DKA) * (DKA ** -0.5)
    v = hi.astype(F32).reshape(B, T, HA, DVA)
    log_lb = jnp.log(lb)
    log_ub = jnp.log1p(-lb)
    outs, finals = [], []
    for d, fpre in enumerate((hf_fwd, hf_bwd)):
        z = fpre.astype(F32)
        logf = jnp.logaddexp(log_lb, log_ub + jax.nn.log_sigmoid(z)).reshape(B, T, HA, DKA)
        k = ((1.0 - lb) * jax.nn.sigmoid(-z)).reshape(B, T, HA, DKA)
        seqs = (q, k, v, logf)
        if d == 1:
            seqs = tuple(jnp.flip(a, axis=1) for a in seqs)
        s_fin, o = _gla_chunk_scan(*seqs, s0[:, d])
        outs.append(jnp.flip(o, axis=1) if d == 1 else o)
        finals.append(s_fin)
    o = _rms(outs[0] + outs[1], norm_w.reshape(HA, DVA)).reshape(B, T, WA)
    return o * jax.nn.silu(hg.astype(F32)), jnp.stack(finals, axis=1)


def _rwkv_scan(r, w, k, v, kk, a, s0):
    def step(S, inp):
        rt, wt, kt, vt, kkt, at = inp
        sa = jnp.einsum('bhvk,bhk->bhv', S, -kkt)
        S = S * wt[:, :, None, :] + sa[..., None] * (kkt * at)[:, :, None, :] + vt[..., None] * kt[:, :, None, :]
        return S, jnp.einsum('bhvk,bhk->bhv', S, rt)

    xs = tuple(jnp.moveaxis(t.astype(F32), 1, 0) for t in (r, w, k, v, kk, a))
    S, o = lax.scan(step, s0.astype(F32), xs)
    return S, jnp.moveaxis(o, 0, 1)


def _rwkv7(rw, grid, mu, w0, w2, a0, a2, g2, k_k, k_a, r_k, ln, s0):
    B, T, _ = rw.shape
    rw = _token_shift(rw, mu, grid)
    r, k, v, wlo_f, wlo_b, alo_f, alo_b, glo = _split(rw, (WB, WB, WB, R_DECAY, R_DECAY, R_AAA, R_AAA, R_GATE))

    def heads(a):
        return a.astype(F32).reshape(B, T, HB, NB)

    r_h, k_h, v_h = heads(r), heads(k), heads(v)
    kk = heads(k * k_k)
    kk = kk / jnp.maximum(jnp.sqrt(jnp.sum(kk * kk, axis=-1, keepdims=True)), 1e-12)
    k_a_h = k_a.astype(F32).reshape(HB, NB)
    outs, finals = [], []
    for d, (wlo, alo) in enumerate(((wlo_f, alo_f), (wlo_b, alo_b))):
        wlog = -jax.nn.softplus(-(w0[d] + jnp.tanh(wlo) @ w2[d]).astype(F32)) - 0.5
        decay = heads(jnp.exp(-jnp.exp(wlog)))
        a_h = heads(jax.nn.sigmoid((a0[d] + alo @ a2[d]).astype(F32)))
        k_d = k_h * (1.0 + (a_h - 1.0) * k_a_h)
        seqs = (r_h, decay, k_d, v_h, kk, a_h)
        if d == 1:
            seqs = tuple(jnp.flip(t, axis=1) for t in seqs)
        s_fin, o = _rwkv_scan(*seqs, s0[:, d])
        outs.append(jnp.flip(o, axis=1) if d == 1 else o)
        finals.append(s_fin)
    o = outs[0] + outs[1]
    mean = jnp.mean(o, axis=-1, keepdims=True)
    var = jnp.mean(jnp.square(o - mean), axis=-1, keepdims=True)
    o = (o - mean) * lax.rsqrt(var + RWKV_LN_EPS) * ln[0].astype(F32).reshape(HB, NB) + ln[1].astype(F32).reshape(HB, NB)
    bonus = jnp.sum(r_h * k_h * r_k.astype(F32).reshape(HB, NB), axis=-1, keepdims=True) * v_h
    o = (o + bonus).reshape(B, T, WB) * (jax.nn.sigmoid(glo) @ g2)
    return o, jnp.stack(finals, axis=1)


def _mlstm_chunk_scan(q, k, v, ig, logf, C0, n0, m0):
    B, T, H, Dh = q.shape
    L = MLSTM_CHUNK
    n = T // L

    def vec(a):
        return a.reshape(B, n, L, H, Dh).transpose(1, 0, 3, 2, 4)

    def sca(a):
        return a.reshape(B, n, L, H).transpose(1, 0, 3, 2)

    causal = jnp.tril(jnp.ones((L, L), dtype=bool))

    def step(carry, inp):
        C, nv, m = carry
        qc, kc, vc, ic, fc = inp
        g = jnp.cumsum(fc, axis=-1)
        a_inter = g + m[..., None]
        d_intra = jnp.where(causal, g[..., :, None] - g[..., None, :] + ic[..., None, :], -jnp.inf)
        m_t = jnp.maximum(a_inter, jnp.max(d_intra, axis=-1))
        w_inter = jnp.exp(a_inter - m_t)
        p = jnp.exp(d_intra - m_t[..., None]) * jnp.einsum('bhtd,bhsd->bhts', qc, kc)
        num = w_inter[..., None] * jnp.einsum('bhtd,bhde->bhte', qc, C) + jnp.einsum('bhts,bhse->bhte', p, vc)
        den = w_inter * jnp.einsum('bhtd,bhd->bht', qc, nv) + jnp.sum(p, axis=-1)
        h = num / jnp.maximum(jnp.abs(den), jnp.exp(-m_t))[..., None]
        g_end = g[..., -1]
        w_src = g_end[..., None] - g + ic
        m_new = jnp.maximum(g_end + m, jnp.max(w_src, axis=-1))
        carry_scale = jnp.exp(g_end + m - m_new)
        w_src = jnp.exp(w_src - m_new[..., None])
        C = carry_scale[..., None, None] * C + jnp.einsum('bhs,bhsd,bhse->bhde', w_src, kc, vc)
        nv = carry_scale[..., None] * nv + jnp.einsum('bhs,bhsd->bhd', w_src, kc)
        return (C, nv, m_new), h

    init = (C0.astype(F32), n0.astype(F32), m0.astype(F32))
    (C, nv, m), h = lax.scan(step, init, (vec(q), vec(k), vec(v), sca(ig), sca(logf)))
    return (C, nv, m), h.transpose(1, 0, 3, 2, 4).reshape(B, T, H, Dh)


def _mlstm(mq, mk, mv, mo, mg, gate_b, norm_w, C0, n0, m0):
    B, T, _ = mq.shape

    def heads(a):
        return a.astype(F32).reshape(B, T, HC, DHC)

    q, k, v = heads(mq), heads(mk) * (DHC ** -0.5), heads(mv)
    gt = mg.astype(F32).reshape(B, T, 4, HC) + gate_b.astype(F32)
    hs, Cs, ns, ms = [], [], [], []
    for d in range(2):
        seqs = (q, k, v, gt[:, :, 2 * d], jax.nn.log_sigmoid(gt[:, :, 2 * d + 1]))
        if d == 1:
            seqs = tuple(jnp.flip(t, axis=1) for t in seqs)
        (C, nv, m), h = _mlstm_chunk_scan(*seqs, C0[:, d], n0[:, d], m0[:, d])
        hs.append(jnp.flip(h, axis=1) if d == 1 else h)
        Cs.append(C)
        ns.append(nv)
        ms.append(m)
    h = _rms(hs[0] + hs[1], norm_w.reshape(HC, DHC)).reshape(B, T, WC)
    return h * jax.nn.sigmoid(mo.astype(F32)), (jnp.stack(Cs, axis=1), jnp.stack(ns, axis=1), jnp.stack(ms, axis=1))


def _mixer(h, l, grid, st, P):
    (hq, hf_f, hf_b, hi, hg, rw, mq, mk, mv, mo, mg, gates) = _split(h @ P['w_in'][l], IN_WIDTHS)
    s_hgrn, s_rwkv, s_C, s_n, s_m = st
    y_a, new_a = _hgrn2(hq, hf_f, hf_b, hi, hg, P['lbs'][l], P['hgrn_norm'][l], s_hgrn)
    y_b, new_b = _rwkv7(rw, grid, P['rwkv_mu'][l], P['rwkv_w0'][l], P['rwkv_w2'][l], P['rwkv_a0'][l],
                        P['rwkv_a2'][l], P['rwkv_g2'][l], P['rwkv_kk'][l], P['rwkv_ka'][l], P['rwkv_rk'][l],
                        P['rwkv_ln'][l], s_rwkv)
    y_c, (new_C, new_n, new_m) = _mlstm(mq, mk, mv, mo, mg, P['mlstm_gate_b'][l], P['mlstm_norm'][l], s_C, s_n, s_m)
    g_a, g_b, g_c = jnp.split(jax.nn.sigmoid(gates.astype(F32)), 3, axis=-1)
    wb = P['w_branch'][l]
    merged = g_a * (y_a @ wb[0]) + g_b * (y_b @ wb[1]) + g_c * (y_c @ wb[2])
    return (merged @ P['w_out'][l]).astype(h.dtype), (new_a, new_b, new_C, new_n, new_m)


def _trunk(x, cond, states, grid, P):
    new = ([], [], [], [], [])
    for l in range(DEPTH):
        mod = (jax.nn.silu(cond) @ P['w_ada'][l] + P['b_ada'][l])[:, None, :]
        sh1, sc1, g1, sh2, sc2, g2 = jnp.split(mod, 6, axis=-1)
        nw = P['norms'][l]
        h = _rms(x, nw[0]) * (1 + sc1) + sh1
        y, st = _mixer(h, l, grid, tuple(s[:, l] for s in states), P)
        x = x + g1 * _rms(y, nw[1])
        h = _rms(x, nw[2]) * (1 + sc2) + sh2
        u, gt = jnp.split(h @ P['w_ffn_in'][l], 2, axis=-1)
        y = (jax.nn.silu(gt) * u) @ P['w_ffn_out'][l]
        x = x + g2 * _rms(y, nw[3])
        for lst, s in zip(new, st):
            lst.append(s)
    return x, tuple(jnp.stack(lst, axis=1) for lst in new)


def setup_inputs(seed: int = 0) -> dict:
    key = jax.random.key(seed)
    ks = jax.random.split(key, 32)

    def nrm(i, shape, scale):
        return jax.random.normal(ks[i], shape, F32) * scale

    return {
        'x_prompt': nrm(0, (BATCH, SEQ, D_MODEL), 1.0),
        'x_sample': nrm(1, (DEC_BATCH, DEC_SEQ, D_MODEL), 1.0),
        'state_hgrn': nrm(2, (DEC_BATCH, DEPTH, 2, HA, DKA, DVA), 0.3),
        'state_rwkv': nrm(3, (DEC_BATCH, DEPTH, 2, HB, NB, NB), 0.3),
        'state_mlstm_C': nrm(4, (DEC_BATCH, DEPTH, 2, HC, DHC, DHC), 0.3),
        'state_mlstm_n': nrm(5, (DEC_BATCH, DEPTH, 2, HC, DHC), 0.3),
        'state_mlstm_m': nrm(6, (DEC_BATCH, DEPTH, 2, HC), 0.5),
        'c': nrm(7, (DEC_BATCH, D_MODEL), 1.0),
        'c_ctx': nrm(8, (D_MODEL,), 1.0),
        'w_ada': nrm(9, (DEPTH, D_MODEL, 6 * D_MODEL), 0.5 * D_MODEL ** -0.5),
        'b_ada': nrm(10, (DEPTH, 6 * D_MODEL), 0.02),
        'norms': 1.0 + nrm(11, (DEPTH, 4, D_MODEL), 0.05),
        'w_in': nrm(12, (DEPTH, D_MODEL, D_IN), D_MODEL ** -0.5),
        'hgrn_lb': nrm(13, (DEPTH, HA * DKA), 1.0),
        'hgrn_norm': 1.0 + nrm(14, (DEPTH, WA), 0.05),
        'rwkv_mu': 0.2 + nrm(15, (DEPTH, 4, RWKV_COLS), 0.05),
        'rwkv_w0': nrm(16, (DEPTH, 2, WB), 0.5),
        'rwkv_w2': nrm(17, (DEPTH, 2, R_DECAY, WB), R_DECAY ** -0.5),
        'rwkv_a0': nrm(18, (DEPTH, 2, WB), 0.1),
        'rwkv_a2': nrm(19, (DEPTH, 2, R_AAA, WB), R_AAA ** -0.5),
        'rwkv_g2': nrm(20, (DEPTH, R_GATE, WB), R_GATE ** -0.5),
        'rwkv_kk': 0.85 + nrm(21, (DEPTH, WB), 0.05),
        'rwkv_ka': 1.0 + nrm(22, (DEPTH, WB), 0.05),
        'rwkv_rk': nrm(23, (DEPTH, WB), 0.1),
        'rwkv_ln': nrm(24, (DEPTH, 2, WB), 0.05) + jnp.array([1.0, 0.0], F32)[None, :, None],
        'mlstm_gate_b': nrm(25, (DEPTH, 4, HC), 0.1) + jnp.array([0.0, 3.0, 0.0, 3.0], F32)[None, :, None],
        'mlstm_norm': 1.0 + nrm(26, (DEPTH, WC), 0.05),
        'w_branch': nrm(27, (DEPTH, 3, WA, D_MODEL), WA ** -0.5),
        'w_out': nrm(28, (DEPTH, D_MODEL, D_MODEL), D_MODEL ** -0.5),
        'w_ffn_in': nrm(29, (DEPTH, D_MODEL, 2 * D_FF), D_MODEL ** -0.5),
        'w_ffn_out': nrm(30, (DEPTH, D_FF, D_MODEL), D_FF ** -0.5),
    }


def reference(x_prompt, x_sample, state_hgrn, state_rwkv, state_mlstm_C, state_mlstm_n, state_mlstm_m,
              c, c_ctx, w_ada, b_ada, norms, w_in, hgrn_lb, hgrn_norm, rwkv_mu, rwkv_w0, rwkv_w2,
              rwkv_a0, rwkv_a2, rwkv_g2, rwkv_kk, rwkv_ka, rwkv_rk, rwkv_ln, mlstm_gate_b, mlstm_norm,
              w_branch, w_out, w_ffn_in, w_ffn_out):
    lbs = jnp.cumsum(jax.nn.softmax(hgrn_lb.astype(F32), axis=0), axis=0)
    lbs = lbs - lbs[0:1]
    P = {'w_ada': w_ada, 'b_ada': b_ada, 'norms': norms, 'w_in': w_in, 'lbs': lbs, 'hgrn_norm': hgrn_norm,
         'rwkv_mu': rwkv_mu, 'rwkv_w0': rwkv_w0, 'rwkv_w2': rwkv_w2, 'rwkv_a0': rwkv_a0, 'rwkv_a2': rwkv_a2,
         'rwkv_g2': rwkv_g2, 'rwkv_kk': rwkv_kk, 'rwkv_ka': rwkv_ka, 'rwkv_rk': rwkv_rk, 'rwkv_ln': rwkv_ln,
         'mlstm_gate_b': mlstm_gate_b, 'mlstm_norm': mlstm_norm, 'w_branch': w_branch, 'w_out': w_out,
         'w_ffn_in': w_ffn_in, 'w_ffn_out': w_ffn_out}
    b = x_prompt.shape[0]
    zero_states = (jnp.zeros((b, DEPTH, 2, HA, DKA, DVA), F32),
                   jnp.zeros((b, DEPTH, 2, HB, NB, NB), F32),
                   jnp.zeros((b, DEPTH, 2, HC, DHC, DHC), F32),
                   jnp.zeros((b, DEPTH, 2, HC, DHC), F32),
                   jnp.zeros((b, DEPTH, 2, HC), F32))
    y_prompt, ctx_states = _trunk(x_prompt, c_ctx[None, :], zero_states, False, P)
    new_hgrn, new_rwkv, new_C, new_n, new_m = ctx_states
    cache = (state_hgrn, state_rwkv, state_mlstm_C, state_mlstm_n, state_mlstm_m)
    y_sample, _ = _trunk(x_sample, c, cache, True, P)
    return (y_prompt, y_sample, new_hgrn, new_rwkv, new_C, new_n, new_m)
```

```python
import numpy as np
import concourse.bass as bass
import concourse.mybir as mybir
from concourse.bass_utils import run_bass_kernel_spmd
from contextlib import ExitStack

F32 = mybir.dt.float32
BF16 = mybir.dt.bfloat16
AF = mybir.ActivationFunctionType
ALU = mybir.AluOpType
AX = mybir.AxisListType

NCORES = 8
TOK = 2048
NTB = 4
TB = 512
NTT = 16
D = 1024
KC = 8
DEPTH = 2
D_IN = 9616
D_FF = 2816
NFF = 22
C_HQ, C_HFF, C_HFB, C_HI, C_HG = 0, 512, 1024, 1536, 2048
C_RW = 2560
C_MQ, C_MK, C_MV, C_MO, C_MG = 4480, 4992, 5504, 6016, 6528
C_GATES = 6544
KAPPA = float(np.exp(-0.5))
NSEG = 8

PC_NW, PC_BADA, PC_LB0, PC_LB1, PC_HNORM, PC_MU, PC_W0, PC_A0, PC_KK, PC_KA, PC_RK, PC_LN0, PC_LN1, PC_MNORM, PC_GBF, PC_GBI = \
    0, 32, 80, 84, 88, 92, 152, 160, 168, 172, 176, 180, 184, 188, 192, 193
NPC = 194
CC_COND, CC_KEEP, CC_MUF, CC_M0, CC_BML, CC_BMR = 0, 8, 9, 13, 29, 60
NCC = 91
CS_ID, CS_ONES, CS_BONES, CS_M64F, CS_M64B, CS_M128F, CS_M128B, CS_RWF, CS_RWB, CS_SELF, CS_SELI, CS_SGN, CS_MDF, CS_MDB = \
    0, 128, 256, 384, 448, 512, 640, 768, 1024, 1280, 1296, 1312, 1313, 1314
NCST = 1315

DBG = {}


class T:
    __slots__ = ('t', 'writer', 'readers')

    def __init__(self, t):
        self.t = t
        self.writer = None
        self.readers = {}

    def __getitem__(self, idx):
        return self.t[idx]


class KB:
    def __init__(self, n_dma_sems=48):
        self.nc = bass.Bass("TRN2", target_bir_lowering=False)
        nc = self.nc
        self.eng = {'pe': nc.tensor, 'act': nc.scalar, 'dve': nc.vector, 'pool': nc.gpsimd, 'sp': nc.sync}
        self.sem = {}
        self.cnt = {}
        for e in ['pe', 'act', 'dve', 'pool']:
            self.sem[e] = nc.alloc_semaphore(name='s_' + e)
            self.cnt[e] = 0
        self.ndma = n_dma_sems
        for i in range(n_dma_sems):
            k = 'd%d' % i
            self.sem[k] = nc.alloc_semaphore(name='s_' + k)
            self.cnt[k] = 0
        self.dma_rr = 0
        self.dma_rr_sw = 0
        self.seen = {e: {} for e in self.eng}
        self.pending = None
        self.snap = {}
        self.nins = 0
        self.out_toks = []
        self.banks = [nc.alloc_psum_tensor('psb%d' % i, [128, 512], F32) for i in range(8)]
        self.pst = [T(None) for _ in range(8)]
        self.ps_rr = 0
        self.psb_rr = 0

    def sb(self, name, shape, dt=F32, stack=None):
        self.nalloc = getattr(self, 'nalloc', 0) + 1
        name = "%s_u%d" % (name, self.nalloc)
        if stack is None:
            return T(self.nc.alloc_sbuf_tensor(name, list(shape), dt))
        return T(stack.enter_context(self.nc.sbuf_tensor(name, list(shape), dt)))

    def ps_small(self):
        i = self.ps_rr
        self.ps_rr = (i + 1) % 16
        b = i % 8
        h = i // 8
        ap = self.banks[b][:, h * 256:h * 256 + 256]
        return [self.pst[b]], ap

    def ps_big(self):
        b = self.psb_rr
        self.psb_rr = (b + 1) % 8
        return [self.pst[b]], self.banks[b][:, :]

    def _wait(self, e, key, val, raw=False):
        if key == e and e == 'pe':
            return
        if self.seen[e].get(key, 0) >= val:
            return
        se = self.seen[e]
        se[key] = val
        sn = self.snap.get((key, val))
        if sn:
            for k2, v2 in sn.items():
                if k2 != e and se.get(k2, 0) < v2:
                    se[k2] = v2
        if self.pending is not None:
            self.pending.append((key, val))
        else:
            self.eng[e].wait_ge(self.sem[key], val)

    def _deps(self, e, reads, writes):
        for r in reads:
            if r.writer is not None:
                self._wait(e, r.writer[0], r.writer[1], raw=True)
        for w in writes:
            if w.writer is not None:
                self._wait(e, w.writer[0], w.writer[1])
            for k, v in w.readers.items():
                self._wait(e, k, v)

    def op(self, e, fn, reads=(), writes=(), inc=True):
        fuse = e in ('act', 'dve', 'pool', 'pe')
        self.pending = [] if fuse else None
        self._deps(e, reads, writes)
        pend = self.pending or []
        self.pending = None
        best = {}
        for k, v in pend:
            best[k] = max(best.get(k, 0), v)
        pend = list(best.items())
        for k, v in pend[:-1]:
            self.eng[e].wait_ge(self.sem[k], v)
        ins = fn(self.eng[e])
        if pend:
            k, v = pend[-1]
            ins.wait_op(self.sem[k], v, "sem-ge")
        if inc:
            self.cnt[e] += 1
            ins.then_inc(self.sem[e], 1)
            c = self.cnt[e]
            self.snap[(e, c)] = dict(self.seen[e])
        else:
            c = self.cnt[e] + 1
        for r in reads:
            r.readers[e] = c
        for w in writes:
            w.writer = (e, c)
            w.readers = {}
        self.nins += 1

    def dma(self, q, out, in_, reads=(), writes=()):
        self._deps(q, reads, writes)
        half = self.ndma // 2
        if q == 'pool':
            k = 'd%d' % (half + self.dma_rr_sw)
            self.dma_rr_sw = (self.dma_rr_sw + 1) % half
        else:
            k = 'd%d' % self.dma_rr
            self.dma_rr = (self.dma_rr + 1) % half
        if self.cnt[k] > 0:
            self._wait(q, k, self.cnt[k])
        ins = self.eng[q].dma_start(out=out, in_=in_)
        self.cnt[k] += 16
        ins.then_inc(self.sem[k], 16)
        tok = (k, self.cnt[k])
        self.snap[tok] = dict(self.seen[q])
        for r in reads:
            r.readers[k] = self.cnt[k]
        for w in writes:
            w.writer = tok
            w.readers = {}
        self.nins += 1
        return tok

    def barrier(self):
        for e in self.eng:
            for k, v in self.cnt.items():
                if v > 0:
                    self._wait(e, k, v, raw=True)

    def mm(self, out, lhsT, rhs, start, stop, R, W, inc=None):
        if inc is None:
            inc = True if DBG.get("allinc", 0) else bool(stop)
        self.op('pe', lambda e: e.matmul(out, lhsT=lhsT, rhs=rhs, start=start, stop=stop), R, W, inc=inc)

    def tr(self, out, in_, ident, R, W):
        self.op('pe', lambda e: e.transpose(out, in_, ident), R, W)

    def act(self, out, in_, func, R, W, bias=None, scale=None, eng='act'):
        kw = {}
        if bias is not None:
            kw['bias'] = bias
        if scale is not None:
            kw['scale'] = scale
        self.op(eng, lambda e: e.activation(out=out, in_=in_, func=func, **kw), R, W)

    def tt(self, out, in0, in1, op, R, W, eng='dve'):
        self.op(eng, lambda e: e.tensor_tensor(out=out, in0=in0, in1=in1, op=op), R, W)

    def ts(self, out, in0, s1, s2, op0, op1, R, W, eng='dve'):
        if op1 is None:
            self.op(eng, lambda e: e.tensor_scalar(out=out, in0=in0, scalar1=s1, scalar2=None, op0=op0), R, W)
        else:
            self.op(eng, lambda e: e.tensor_scalar(out=out, in0=in0, scalar1=s1, scalar2=s2, op0=op0, op1=op1), R, W)

    def stt(self, out, in0, scalar, in1, op0, op1, R, W):
        self.op('dve', lambda e: e.scalar_tensor_tensor(out=out, in0=in0, scalar=scalar, in1=in1, op0=op0, op1=op1), R, W)

    def cp(self, out, in_, R, W, eng='dve'):
        if eng == 'act':
            self.act(out, in_, AF.Copy, R, W)
        else:
            self.op(eng, lambda e: e.tensor_copy(out=out, in_=in_), R, W)

    def memset(self, ap, val, W, eng='pool'):
        self.op(eng, lambda e: e.memset(ap, val), [], W)


class Prog:
    def __init__(self):
        self.kb = KB()
        self.nc = self.kb.nc
        self.taps = {}

    def din(self, name, shape):
        return self.nc.dram_tensor(name, list(shape), F32, kind="ExternalInput").ap()

    def dout(self, name, shape):
        return self.nc.dram_tensor(name, list(shape), F32, kind="ExternalOutput").ap()

    def tap(self, name, t, ap, shape):
        if not DBG.get('taps'):
            return
        o = self.dout('tap_' + name, shape)
        self.kb.out_toks.append(self.kb.dma('pool', o, ap, reads=[t]))

    def wl(self, segs):
        t = self.wring[self.wr_i]
        self.wr_i = (self.wr_i + 1) % len(self.wring)
        for dst_fn, src in segs:
            self.kb.dma('pool', dst_fn(t.t), src, writes=[t])
        return t

    def w_in_cols(self, l, cols, width=128):
        src = self.w_in[l].rearrange("(kc p) c -> p kc c", p=128)
        segs = []
        if DBG.get('contig') and len(cols) > 1 and all(cols[j + 1] == cols[j] + width for j in range(len(cols) - 1)):
            n = len(cols) * width
            return self.wl([(lambda tl: tl[:, :].rearrange("p (k c) -> p k c", c=512)[:, :, 0:n], src[:, :, cols[0]:cols[0] + n])])
        for j, c0 in enumerate(cols):
            segs.append((lambda tl, j=j: tl[:, :].rearrange("p (k c) -> p k c", c=512)[:, :, j * width:(j + 1) * width],
                         src[:, :, c0:c0 + width]))
        return self.wl(segs)

    def wv(self, t):
        return t.t[:, :].rearrange("p (k c) -> p k c", c=512)

    def proj(self, wt, j, tb, width=128, M=128):
        kb = self.kb
        pt, ps = kb.ps_big()
        wvw = self.wv(wt)
        h = self.hT[tb]
        for kc in range(KC):
            kb.mm(ps[0:M, :], wvw[:, kc, j * width:j * width + M], h.t[:, kc, :], kc == 0, kc == KC - 1, [wt, h], pt)
        return pt, ps[0:M, :]

    def build(self):
        kb, nc = self.kb, self.nc
        self.xT_in = self.din("xT", [D, TOK])
        self.pc_in = self.din("pc", [DEPTH, 128, NPC])
        self.cc_in = self.din("cc", [128, NCC])
        self.cst_in = self.din("cst", [128, NCST])
        self.w_ada = self.din("w_ada", [DEPTH, D, 6 * D])
        self.w_in = self.din("w_in", [DEPTH, D, D_IN])
        self.w2 = self.din("rwkv_w2", [DEPTH, 2, 64, 512])
        self.a2 = self.din("rwkv_a2", [DEPTH, 2, 64, 512])
        self.g2 = self.din("rwkv_g2", [DEPTH, 128, 512])
        self.w_branch = self.din("w_branch", [DEPTH, 3, 512, D])
        self.w_out = self.din("w_out", [DEPTH, D, D])
        self.w_ffn_in = self.din("w_ffn_in", [DEPTH, D, 2 * D_FF])
        self.w_ffn_out = self.din("w_ffn_out", [DEPTH, D_FF, D])
        self.hs0 = self.din("hs0", [DEPTH, 2, 4, 128, 128])
        self.rs0 = self.din("rs0", [DEPTH, 2, 8, 64, 64])
        self.cs0 = self.din("cs0", [DEPTH, 2, 4, 128, 128])
        self.ns0 = self.din("ns0", [DEPTH, 2, 4, 128, 1])
        self.yT_out = self.dout("yT", [D, TOK])
        self.so_h = self.dout("so_h", [DEPTH, 2, NSEG, 4, 128, 128])
        self.so_r = self.dout("so_r", [DEPTH, 2, NSEG, 8, 64, 64])
        self.so_c = self.dout("so_c", [DEPTH, 2, NSEG, 4, 128, 128])
        self.so_n = self.dout("so_n", [DEPTH, 2, NSEG, 4, 128, 1])
        self.so_m = self.dout("so_m", [DEPTH, 16, NSEG])
        self.xs = nc.dram_tensor("xspill", [NTB, 128, KC, TB], F32, kind="Internal").ap()

        self.cst = kb.sb("cst_sb", [128, NCST])
        self.cstb = kb.sb("cstb", [128, NCST], BF16)
        self.pcl = [kb.sb("pcl%d" % l, [128, NPC]) for l in range(DEPTH)]
        self.cc = kb.sb("cc_sb", [128, NCC])
        kb.dma('sp', self.cst[:], self.cst_in, writes=[self.cst])
        kb.dma('pool', self.cstb[:], self.cst_in, writes=[self.cstb])
        for l in range(DEPTH):
            kb.dma('sp', self.pcl[l][:], self.pc_in[l], writes=[self.pcl[l]])
        kb.dma('sp', self.cc[:], self.cc_in, writes=[self.cc])
        self.wring = [kb.sb("wring%d" % i, [128, 4096], BF16) for i in range(3)]
        self.wr_i = 0
        self.hT = [kb.sb("hT%d" % tb, [128, KC, TB], BF16) for tb in range(NTB)]
        self.regM = kb.sb("regM", [128, 4 * KC * TB], BF16)
        self.merged = [T(self.regM.t[:, tb * KC * TB:(tb + 1) * KC * TB].rearrange("p (k t) -> p k t", t=TB)) for tb in range(NTB)]
        self.modT = [kb.sb("modT%d" % l, [128, 48]) for l in range(DEPTH)]
        self.cols = [kb.sb("cols%d" % l, [128, 64]) for l in range(DEPTH)]
        self.mue = [kb.sb("mue%d" % l, [128, 5 * 15]) for l in range(DEPTH)]
        self.scond = kb.sb("scond", [128, KC], BF16)
        self.rstd = [kb.sb("rstd%d" % i, [128, TB]) for i in range(2)]
        self.rs_i = 0
        self.tmpA = [kb.sb("tmpA%d" % i, [128, TB]) for i in range(2)]
        self.ta_i = 0

        self.ident_b = self.cstb.t[:, CS_ID:CS_ID + 128]
        self.ones_b = self.cstb.t[:, CS_ONES:CS_ONES + 128]
        self.bones_b = self.cstb.t[:, CS_BONES:CS_BONES + 128]

        kb.act(self.scond[:], self.cc.t[:, CC_COND:CC_COND + 8], AF.Silu, [self.cc], [self.scond])

        with ExitStack() as st0:
            self.xT = [kb.sb("xT%d" % tb, [128, KC, TB], F32, st0) for tb in range(NTB)]
            xin = self.xT_in.rearrange("(kc p) t -> p kc t", p=128)
            for tb in range(NTB):
                kb.dma('sp', self.xT[tb][:], xin[:, :, tb * TB:(tb + 1) * TB], writes=[self.xT[tb]])
            self.compute_mod(0)
            self.layer_cols(0)
            self.norm_to_h(0, 0)
            self.spill_x()
            kb.barrier()
        for l in range(DEPTH):
            self.mixer(l)
            with ExitStack() as st1:
                self.xT = [kb.sb("xT%d_%d" % (tb, l), [128, KC, TB], F32, st1) for tb in range(NTB)]
                self.reload_x()
                if l + 1 < DEPTH:
                    self.compute_mod(l + 1)
                    self.layer_cols(l + 1)
                self.out_proj(l)
                self.ffn(l, st1)
                if l + 1 < DEPTH:
                    self.norm_to_h(l + 1, 0)
                    self.spill_x()
                else:
                    yo = self.yT_out.rearrange("(kc p) t -> p kc t", p=128)
                    for tb in range(NTB):
                        kb.out_toks.append(kb.dma('sp', yo[:, :, tb * TB:(tb + 1) * TB], self.xT[tb][:], reads=[self.xT[tb]]))
                kb.barrier()
        for k, v in kb.out_toks:
            kb._wait('sp', k, v)
        return nc

    def compute_mod(self, l):
        kb = self.kb
        pt, ps = kb.ps_small()
        src = self.w_ada[l].rearrange("(kc p) c -> p kc c", p=128)
        for blk in range(12):
            wt = self.wl([(lambda tl: tl[:, :].rearrange("p (k c) -> p k c", c=512), src[:, :, blk * 512:(blk + 1) * 512])])
            wv = self.wv(wt)
            for jj in range(4):
                j = blk * 4 + jj
                for kc in range(KC):
                    kb.mm(ps[:, j:j + 1], wv[:, kc, jj * 128:(jj + 1) * 128], self.scond.t[:, kc:kc + 1], kc == 0, kc == KC - 1,
                          [wt, self.scond], pt)
        kb.tt(self.modT[l][:], ps[:, 0:48], self.pcl[l].t[:, PC_BADA:PC_BADA + 48], ALU.add, pt + [self.pcl[l]], [self.modT[l]])

    def layer_cols(self, l):
        kb = self.kb
        m, c, p = self.modT[l], self.cols[l], self.pcl[l]
        kb.stt(c.t[:, 0:8], m.t[:, 8:16], 1.0, p.t[:, PC_NW:PC_NW + 8], ALU.add, ALU.mult, [m, p], [c])
        kb.tt(c.t[:, 8:16], m.t[:, 16:24], p.t[:, PC_NW + 8:PC_NW + 16], ALU.mult, [m, p], [c])
        kb.stt(c.t[:, 16:24], m.t[:, 32:40], 1.0, p.t[:, PC_NW + 16:PC_NW + 24], ALU.add, ALU.mult, [m, p], [c])
        kb.tt(c.t[:, 24:32], m.t[:, 40:48], p.t[:, PC_NW + 24:PC_NW + 32], ALU.mult, [m, p], [c])
        kb.tt(c.t[:, 32:36], p.t[:, PC_LB1:PC_LB1 + 4], p.t[:, PC_LB0:PC_LB0 + 4], ALU.subtract, [p], [c])
        kb.act(c.t[:, 32:36], c.t[:, 32:36], AF.Sigmoid, [c], [c])
        if l == 0:
            kb.ts(c.t[:, 32:36], c.t[:, 32:36], 0.0, None, ALU.mult, None, [c], [c])
        kb.ts(c.t[:, 36:40], c.t[:, 32:36], -1.0, 1.0, ALU.mult, ALU.add, [c], [c])
        me = self.mue[l]
        for j in range(4):
            kb.ts(me.t[:, j * 15:(j + 1) * 15], p.t[:, PC_MU + j * 15:PC_MU + (j + 1) * 15], self.cc.t[:, CC_MUF + j:CC_MUF + j + 1], None,
                  ALU.mult, None, [p, self.cc], [me])
        kb.tt(me.t[:, 60:75], me.t[:, 0:15], me.t[:, 15:30], ALU.add, [me], [me])
        kb.tt(me.t[:, 60:75], me.t[:, 60:75], me.t[:, 30:45], ALU.add, [me], [me])
        kb.tt(me.t[:, 60:75], me.t[:, 60:75], me.t[:, 45:60], ALU.add, [me], [me])
        kb.ts(me.t[:, 60:75], me.t[:, 60:75], -1.0, 1.0, ALU.mult, ALU.add, [me], [me])

    def rstd_of(self, src_t, src_ap, nk, scale, eps, sq, sq_ap, lhs=None):
        kb = self.kb
        kb.act(sq_ap, src_ap, AF.Square, [src_t], [sq])
        pt, ps = kb.ps_big()
        for kc in range(nk):
            kb.mm(ps, self.ones_b if lhs is None else lhs, sq_ap[:, kc, :], kc == 0, kc == nk - 1, [self.cstb, sq], pt)
        r = self.rstd[self.rs_i]
        self.rs_i ^= 1
        kb.act(r[:], ps, AF.Sqrt, pt, [r], bias=eps, scale=scale)
        kb.op('dve', lambda e: e.reciprocal(out=r[:], in_=r[:]), [r], [r])
        return r

    def norm_to_h(self, l, which):
        kb = self.kb
        c, m = self.cols[l], self.modT[l]
        a0 = 0 if which == 0 else 16
        s0 = 0 if which == 0 else 24
        for tb in range(NTB):
            x = self.xT[tb]
            r = self.rstd_of(x, x[:], KC, 1.0 / D, 1e-6, self.hT[tb], self.hT[tb][:])
            for kc in range(KC):
                tmp = self.tmpA[self.ta_i]
                self.ta_i ^= 1
                kb.stt(tmp[:], x.t[:, kc, :], c.t[:, a0 + kc:a0 + kc + 1], r[:], ALU.mult, ALU.mult, [x, c, r], [tmp])
                kb.act(self.hT[tb].t[:, kc, :], tmp[:], AF.Identity, [tmp, m], [self.hT[tb]], bias=m.t[:, s0 + kc:s0 + kc + 1])

    def spill_x(self):
        for tb in range(NTB):
            self.kb.dma('sp', self.xs[tb], self.xT[tb][:], reads=[self.xT[tb]])

    def reload_x(self):
        self.kb.barrier()
        for tb in range(NTB):
            self.kb.dma('sp', self.xT[tb][:], self.xs[tb], writes=[self.xT[tb]])

    def resid_update(self, l, tb, yt, gcol0):
        kb = self.kb
        c = self.cols[l]
        r = self.rstd_of(yt, yt[:], KC, 1.0 / D, 1e-6, self.hT[tb], self.hT[tb][:])
        x = self.xT[tb]
        for kc in range(KC):
            tmp = self.tmpA[self.ta_i]
            self.ta_i ^= 1
            kb.stt(tmp[:], yt.t[:, kc, :], c.t[:, gcol0 + kc:gcol0 + kc + 1], r[:], ALU.mult, ALU.mult, [yt, c, r], [tmp])
            kb.tt(x.t[:, kc, :], x.t[:, kc, :], tmp[:], ALU.add, [x, tmp], [x], eng='pool')

    def out_proj(self, l):
        kb = self.kb
        with ExitStack() as st:
            ytmp = kb.sb("ytmp_o%d" % l, [128, KC, TB], F32, st)
            src = self.w_out[l].rearrange("(kc p) c -> p kc c", p=128)
            wts = [self.wl([(lambda tl: tl[:, :].rearrange("p (k c) -> p k c", c=512), src[:, :, half * 512:(half + 1) * 512])])
                   for half in range(2)]
            for tb in range(NTB):
                mg = self.merged[tb]
                for half in range(2):
                    wt = wts[half]
                    wv = self.wv(wt)
                    for mm_ in range(4):
                        m = half * 4 + mm_
                        pt, ps = kb.ps_big()
                        for kc in range(KC):
                            kb.mm(ps, wv[:, kc, mm_ * 128:(mm_ + 1) * 128], mg.t[:, kc, :], kc == 0, kc == KC - 1, [wt, mg], pt)
                        kb.cp(ytmp.t[:, m, :], ps, pt, [ytmp], eng='act')
                self.resid_update(l, tb, ytmp, 8)
            kb.barrier()

    def ffn(self, l, st1):
        kb = self.kb
        self.norm_to_h(l, 1)
        with ExitStack() as st:
            actB = kb.sb("ffn_actB%d" % l, [128, NFF - 8, 2 * TB], BF16, st)
            sgt = [kb.sb("ffn_sg%d_%d" % (l, i), [128, TB], F32, st) for i in range(2)]
            kb.barrier()
            ytmp = T(self.regM.t[:, 0:2 * KC * TB].bitcast(F32).rearrange("p (k t) -> p k t", t=TB))
            actA = T(self.regM.t[:, 2 * KC * TB:4 * KC * TB].rearrange("p (j t) -> p j t", t=2 * TB))

            def act_of(j):
                return (actA, actA.t[:, j, :]) if j < 8 else (actB, actB.t[:, j - 8, :])
            wo_src = self.w_ffn_out[l].rearrange("(j p) c -> p j c", p=128)
            for half in range(2):
                for jg in range(NFF // 2):
                    j0 = jg * 2
                    wt = self.w_cols(self.w_ffn_in[l], [j0 * 128, (j0 + 1) * 128, D_FF + j0 * 128, D_FF + (j0 + 1) * 128])
                    for jj in range(2):
                        j = j0 + jj
                        at_, aap = act_of(j)
                        for tbl in range(2):
                            tb = half * 2 + tbl
                            ptu, pu = self.proj(wt, jj, tb)
                            ptg, pg = self.proj(wt, 2 + jj, tb)
                            sg = sgt[tbl]
                            kb.act(sg[:], pg, AF.Silu, ptg, [sg])
                            kb.tt(aap[:, tbl * TB:(tbl + 1) * TB], pu, sg[:], ALU.mult, ptu + [sg], [at_])
                for tbl in range(2):
                    tb = half * 2 + tbl
                    accs = [kb.ps_big() for _ in range(8)]
                    for jg in range(6):
                        nj = 4 if jg < 5 else 2
                        wt = self.wl([(lambda tl, nj=nj: tl[:, 0:nj * 1024].rearrange("p (j c) -> p j c", c=1024), wo_src[:, jg * 4:jg * 4 + nj, :])])
                        wv = wt.t[:, :].rearrange("p (j c) -> p j c", c=1024)
                        for jj in range(nj):
                            j = jg * 4 + jj
                            at_, aap = act_of(j)
                            for m in range(8):
                                kb.mm(accs[m][1], wv[:, jj, m * 128:(m + 1) * 128], aap[:, tbl * TB:(tbl + 1) * TB], j == 0, j == NFF - 1,
                                      [wt, at_], accs[m][0], inc=(m == 7 or j == NFF - 1))
                    for m in range(8):
                        kb.cp(ytmp.t[:, m, :], accs[m][1], accs[m][0], [ytmp], eng='act' if m % 2 else 'dve')
                    self.resid_update(l, tb, ytmp, 24)
            kb.barrier()

    def w_cols(self, wsrc, cols, width=128):
        src = wsrc.rearrange("(kc p) c -> p kc c", p=128)
        segs = []
        if DBG.get('contig') and len(cols) > 1 and all(cols[j + 1] == cols[j] + width for j in range(len(cols) - 1)):
            n = len(cols) * width
            return self.wl([(lambda tl: tl[:, :].rearrange("p (k c) -> p k c", c=512)[:, :, 0:n], src[:, :, cols[0]:cols[0] + n])])
        for j, c0 in enumerate(cols):
            segs.append((lambda tl, j=j: tl[:, :].rearrange("p (k c) -> p k c", c=512)[:, :, j * width:(j + 1) * width],
                         src[:, :, c0:c0 + width]))
        return self.wl(segs)

    def mixer(self, l):
        kb = self.kb
        with ExitStack() as st:
            self.rmF = kb.sb("rmF%d" % l, [128, TB], BF16, st)
            self.rmB = kb.sb("rmB%d" % l, [128, TB], BF16, st)
            self.ycur = [kb.sb("ycur%d_%d" % (l, c), [128, TOK], BF16, st) for c in range(4)]
            self.wbt = kb.sb("wbt%d" % l, [128, 4096], BF16, st)
            if DBG.get('stub_mixer'):
                for tb in range(NTB):
                    kb.memset(self.merged[tb][:], 0.0, [self.merged[tb]])
                kb.barrier()
                return
            nbr = DBG.get('branches', 'ABC')
            first = True
            for i, nm in enumerate('ABC'):
                if nm not in nbr:
                    continue
                with ExitStack() as sb_:
                    if nm == 'A':
                        self.branch_hgrn(l, sb_)
                    elif nm == 'B':
                        self.branch_rwkv(l, sb_)
                    else:
                        self.branch_mlstm(l, sb_)
                    kb.barrier()
                for c in range(4):
                    self.tap("y%s_l%d_c%d" % (nm, l, c), self.ycur[c], self.ycur[c][:], [128, TOK])
                self.merge(l, i, first)
                first = False
            kb.barrier()

    def make_rmask(self, L):
        kb = self.kb
        kb.memset(self.rmF[:], 1.0, [self.rmF])
        kb.memset(self.rmB[:], 1.0, [self.rmB])
        vF = self.rmF.t[:, :].rearrange("p (a b) -> p a b", b=L)
        vB = self.rmB.t[:, :].rearrange("p (a b) -> p a b", b=L)
        kb.memset(vF[:, :, 0:1], 0.0, [self.rmF])
        kb.memset(vB[:, :, L - 1:L], 0.0, [self.rmB])

    def cumsum_chunks(self, out_t, in_t, d, np_=128):
        kb = self.kb
        for tb in range(NTB):
            sl = slice(tb * TB, (tb + 1) * TB)
            if d == 0:
                kb.op('dve', lambda e: e.tensor_tensor_scan(out=out_t.t[0:np_, sl], data0=self.rmF.t[0:np_, :], data1=in_t.t[0:np_, sl],
                                                            initial=0.0, op0=ALU.mult, op1=ALU.add), [self.rmF, in_t], [out_t])
            else:
                kb.op('dve', lambda e: e.tensor_tensor_scan(out=out_t.t[0:np_, sl][:, ::-1], data0=self.rmB.t[0:np_, ::-1],
                                                            data1=in_t.t[0:np_, sl][:, ::-1],
                                                            initial=0.0, op0=ALU.mult, op1=ALU.add), [self.rmB, in_t], [out_t])

    def to_tm(self, dst_t, dst_ap, src_t, src_ap, eng='act'):
        kb = self.kb
        pt, ps = kb.ps_small()
        psb = ps.bitcast(BF16)[:, 0:128]
        kb.tr(psb, src_ap, self.ident_b, [src_t, self.cstb], pt)
        kb.cp(dst_ap, psb, pt, [dst_t], eng=eng)

    def head_post(self, l, oacc, normcol, gate_t, ydst, sq):
        kb = self.kb
        for tb in range(NTB):
            o = oacc[tb]
            r = self.rstd_of(o, o.t[:, :].rearrange("p (k t) -> p k t", k=1), 1, 1.0 / 128, 1e-6, sq, sq.t[:, :].rearrange("p (k t) -> p k t", k=1))
            tmp = self.tmpA[self.ta_i]
            self.ta_i ^= 1
            kb.stt(tmp[:], o[:], normcol, r[:], ALU.mult, ALU.mult, [o, self.pcl[l], r], [tmp])
            kb.tt(ydst.t[:, tb * TB:(tb + 1) * TB], tmp[:], gate_t.t[:, tb * TB:(tb + 1) * TB], ALU.mult, [tmp, gate_t], [ydst])

    def gla_run(self, G, st):
        kb = self.kb
        NV = G['NV']
        den = G.get('den')
        Sall = [kb.sb("g_Sall%d" % d, [128, 16, NV], BF16, st) for d in range(2)]
        if den:
            drow = [kb.sb("g_drow%d" % d, [1, 128], F32, st) for d in range(2)]
            rrep = [kb.sb("g_rrep%d" % d, [128, 128], F32, st) for d in range(2)]
            numr = [kb.sb("g_numr%d" % d, [128, 128], F32, st) for d in range(2)]
        ones_row = self.cst.t[0:1, CS_ONES:CS_ONES + 128]
        ones_colb = self.cstb.t[:, CS_ONES:CS_ONES + 1]

        def passA(d, tile, bi):
            KT, QT, Vtm, Ktm, ATs = G['KT'][d], G['QT'][d], G['Vtm'], G['Ktm'][d][bi], G['ATs'][d][bi]
            S32, e1 = G['S32'][d], G['e1'][d]
            tsl = slice(tile * 128, (tile + 1) * 128)
            KTe = G['KTe'][d]
            self.to_tm(Ktm, Ktm[:], KTe, KTe.t[:, tsl])
            pt, ps = kb.ps_small()
            kb.mm(ps[:, 0:128], KT.t[:, tsl], QT.t[:, tsl], True, True, [KT, QT], pt)
            kb.tt(ATs[:], ps[:, 0:128], G['mask'][d], ALU.mult, pt + [self.cst], [ATs])
            order = range(4) if d == 0 else range(3, -1, -1)
            for q in order:
                ch = tile * 4 + q
                s_cur = S32[G['cur'][d]]
                s_nxt = S32[1 - G['cur'][d]]
                pt2, ps2 = kb.ps_small()
                rs_ = slice(32 * q, 32 * q + 32)
                kb.op('pe', lambda e: e.matmul(ps2[:, 0:NV], lhsT=Ktm.t[rs_, :], rhs=Vtm.t[rs_, tile, 0:NV], start=True, stop=True,
                                               tile_position=(32 * q, 0)), [Ktm, Vtm], pt2)
                kb.cp(Sall[d].t[:, ch % 16, :], s_cur[:], [s_cur], [Sall[d]], eng='act')
                ec = e1.t[:, ch:ch + 1]
                kb.stt(s_nxt[:], s_cur[:], ec, ps2[:, 0:NV], ALU.mult, ALU.add, pt2 + [e1, s_cur], [s_nxt])
                G['cur'][d] = 1 - G['cur'][d]
                seg_end = (d == 0 and tile % 2 == 1 and q == 3) or (d == 1 and tile % 2 == 0 and q == 0)
                if seg_end:
                    G['seg_out'](d, tile // 2, s_nxt)
                    nx2 = S32[1 - G['cur'][d]]
                    kb.ts(nx2[:], s_nxt[:], self.cc.t[:, CC_KEEP:CC_KEEP + 1], None, ALU.mult, None, [s_nxt, self.cc], [nx2])
                    G['cur'][d] = 1 - G['cur'][d]

        def passB(d, tile, bi):
            QT, Vtm, ATs = G['QT'][d], G['Vtm'], G['ATs'][d][bi]
            pto, pso = kb.ps_small()
            kb.mm(pso[:, 0:128], Vtm.t[:, tile, 0:128], ATs[:], True, False, [Vtm, ATs], pto)
            for q in range(4):
                ch = tile * 4 + q
                kb.mm(pso[:, q * 32:q * 32 + 32], Sall[d].t[:, ch % 16, 0:128], QT.t[:, ch * 32:ch * 32 + 32], False, q == 3, [Sall[d], QT], pto)
            ob = G['oacc'][tile // 4]
            osl = slice((tile % 4) * 128, (tile % 4) * 128 + 128)
            if not den:
                kb.tt(ob.t[:, osl], ob.t[:, osl], pso[:, 0:128], ALU.add, [ob] + pto, [ob])
                return
            ptd, psd = kb.ps_small()
            kb.mm(psd[0:1, 0:128], ones_colb, ATs[:], True, False, [self.cstb, ATs], ptd)
            for q in range(4):
                ch = tile * 4 + q
                kb.mm(psd[0:1, q * 32:q * 32 + 32], Sall[d].t[:, ch % 16, 128:129], QT.t[:, ch * 32:ch * 32 + 32], False, q == 3, [Sall[d], QT], ptd)
            dr = drow[d]
            kb.act(dr[:], psd[0:1, 0:128], AF.Abs, ptd, [dr])
            kb.ts(dr[:], dr[:], 1.0, None, ALU.max, None, [dr], [dr])
            kb.op('dve', lambda e: e.reciprocal(out=dr[:], in_=dr[:]), [dr], [dr])
            ptr, psr = kb.ps_small()
            kb.mm(psr[:, 0:128], ones_row, dr[:], True, True, [self.cst, dr], ptr)
            kb.cp(rrep[d][:], psr[:, 0:128], ptr, [rrep[d]], eng='act')
            kb.tt(numr[d][:], pso[:, 0:128], rrep[d][:], ALU.mult, pto + [rrep[d]], [numr[d]])
            kb.tt(ob.t[:, osl], ob.t[:, osl], numr[d][:], ALU.add, [ob, numr[d]], [ob], eng='pool')

        tiles = [list(range(NTT)), list(range(NTT - 1, -1, -1))]
        for step in range(NTT + 1):
            for d in range(2):
                if step < NTT:
                    passA(d, tiles[d][step], step % 2)
                if step >= 1:
                    passB(d, tiles[d][step - 1], (step - 1) % 2)

    def branch_hgrn(self, l, st):
        kb = self.kb
        c, p = self.cols[l], self.pcl[l]
        self.make_rmask(32)
        qT = kb.sb("h_qT", [128, TOK], BF16, st)
        QT = [kb.sb("h_QT%d" % d, [128, TOK], BF16, st) for d in range(2)]
        KT = [kb.sb("h_KT%d" % d, [128, TOK], BF16, st) for d in range(2)]
        KTe = [kb.sb("h_KTe%d" % d, [128, TOK], BF16, st) for d in range(2)]
        vi = kb.sb("h_vi", [128, TB], BF16, st)
        Vtm = kb.sb("h_Vtm", [128, NTT, 128], BF16, st)
        gsil = kb.sb("h_gsil", [128, TOK], BF16, st)
        oacc = [kb.sb("h_oacc%d" % tb, [128, TB], F32, st) for tb in range(NTB)]
        G = {'KT': KT, 'QT': QT, 'KTe': KTe, 'Vtm': Vtm, 'oacc': oacc, 'NV': 128,
             'S32': [[kb.sb("h_S32_%d_%d" % (d, i), [128, 128], F32, st) for i in range(2)] for d in range(2)],
             'e1': [kb.sb("h_e1_%d" % d, [128, 64], F32, st) for d in range(2)],
             'Ktm': [[kb.sb("h_Ktm%d_%d" % (d, i), [128, 128], BF16, st) for i in range(2)] for d in range(2)],
             'ATs': [[kb.sb("h_ATs%d_%d" % (d, i), [128, 128], BF16, st) for i in range(2)] for d in range(2)],
             'mask': [self.cst.t[:, CS_M128F:CS_M128F + 128], self.cst.t[:, CS_M128B:CS_M128B + 128]]}
        hpre = None
        for hd in range(4):
            with ExitStack() as sprep:
                sg = kb.sb("h_sg", [128, TOK], F32, sprep)
                bc = kb.sb("h_bc", [128, TOK], F32, sprep)
                et = kb.sb("h_et", [128, TOK], BF16, sprep)
                kT = kb.sb("h_kT", [128, TOK], BF16, sprep)
                if hpre is not None:
                    wt1, wt2 = hpre
                else:
                    wt1 = self.w_in_cols(l, [C_HQ + hd * 128, C_HFF + hd * 128, C_HFB + hd * 128, C_HI + hd * 128])
                    wt2 = self.w_in_cols(l, [C_HG + hd * 128])
                hpre = None
                lbc = c.t[:, 32 + hd:33 + hd]
                omlc = c.t[:, 36 + hd:37 + hd]
                for tb in range(NTB):
                    sl = slice(tb * TB, (tb + 1) * TB)
                    pt, ps = self.proj(wt1, 0, tb)
                    kb.act(qT.t[:, sl], ps, AF.Copy, pt, [qT], scale=float(128 ** -0.5))
                    pt, ps = self.proj(wt1, 3, tb)
                    kb.cp(vi[:], ps, pt, [vi], eng='dve')
                    for ti in range(4):
                        self.to_tm(Vtm, Vtm.t[:, tb * 4 + ti, :], vi, vi.t[:, ti * 128:(ti + 1) * 128])
                    pt, ps = self.proj(wt2, 0, tb)
                    kb.act(gsil.t[:, sl], ps, AF.Silu, pt, [gsil])
                for d in range(2):
                    for tb in range(NTB):
                        sl = slice(tb * TB, (tb + 1) * TB)
                        pt, ps = self.proj(wt1, 1 + d, tb)
                        kb.act(sg.t[:, sl], ps, AF.Sigmoid, pt, [sg])
                        kb.ts(sg.t[:, sl], sg.t[:, sl], omlc, lbc, ALU.mult, ALU.add, [sg, c], [sg])
                        kb.ts(kT.t[:, sl], sg.t[:, sl], -1.0, 1.0, ALU.mult, ALU.add, [sg], [kT])
                    kb.act(sg[:], sg[:], AF.Ln, [sg], [sg])
                    self.cumsum_chunks(bc, sg, d)
                    bcv = bc.t[:, :].rearrange("p (a b) -> p a b", b=32)
                    bend = bcv[:, :, 31:32] if d == 0 else bcv[:, :, 0:1]
                    kb.act(G['e1'][d].t[:, :].rearrange("p (a b) -> p a b", b=1), bend, AF.Exp, [bc], [G['e1'][d]])
                    kb.act(et[:], bc[:], AF.Exp, [bc], [et])
                    kb.tt(QT[d][:], qT[:], et[:], ALU.mult, [qT, et], [QT[d]])
                    kb.act(et[:], bc[:], AF.Exp, [bc], [et], scale=-1.0)
                    kb.tt(KT[d][:], kT[:], et[:], ALU.mult, [kT, et], [KT[d]])
                    kb.tt(KTe[d].t[:, :].rearrange("p (a b) -> p a b", b=32), KT[d].t[:, :].rearrange("p (a b) -> p a b", b=32),
                          G['e1'][d].t[:, :].rearrange("p (a b) -> p a b", b=1).to_broadcast([128, 64, 32]), ALU.mult,
                          [KT[d], G['e1'][d]], [KTe[d]])
                kb.barrier()
            if hd < 3:
                n_ = hd + 1
                hpre = (self.w_in_cols(l, [C_HQ + n_ * 128, C_HFF + n_ * 128, C_HFB + n_ * 128, C_HI + n_ * 128]),
                        self.w_in_cols(l, [C_HG + n_ * 128]))
            for tb in range(NTB):
                kb.memset(oacc[tb][:], 0.0, [oacc[tb]])
            G['cur'] = [0, 0]
            for d in range(2):
                kb.dma('sp', G['S32'][d][0][:], self.hs0[l, d, hd], writes=[G['S32'][d][0]])

            def seg_out(d, seg, s32, hd=hd):
                kb.out_toks.append(kb.dma('sp', self.so_h[l, d, seg, hd], s32[:], reads=[s32]))
            G['seg_out'] = seg_out
            with ExitStack() as sg_:
                self.gla_run(G, sg_)
                kb.barrier()
            self.head_post(l, oacc, p.t[:, PC_HNORM + hd:PC_HNORM + hd + 1], gsil, self.ycur[hd], vi)

    def branch_mlstm(self, l, st):
        kb = self.kb
        c, p, cs = self.cols[l], self.pcl[l], self.cst
        Z = kb.sb("m_Z", [16, TOK], F32, st)
        mfin = kb.sb("m_mfin", [16, NSEG], F32, st)
        R16 = slice(0, 16)
        mdF, mdB, sgn = cs.t[R16, CS_MDF:CS_MDF + 1], cs.t[R16, CS_MDB:CS_MDB + 1], cs.t[R16, CS_SGN:CS_SGN + 1]
        with ExitStack() as s2:
            A = kb.sb("m_A", [16, TOK], F32, s2)
            B = kb.sb("m_B", [16, TOK], F32, s2)
            C = kb.sb("m_C", [16, TOK], F32, s2)
            Dd = kb.sb("m_D", [16, TOK], F32, s2)
            sm = [kb.sb("m_sm%d" % i, [16, NSEG], F32, s2) for i in range(4)]
            wt = self.w_in_cols(l, [C_MG])
            for tb in range(NTB):
                sl = slice(tb * TB, (tb + 1) * TB)
                pt, ps = self.proj(wt, 0, tb, M=16)
                kb.cp(A.t[:, sl], ps, pt, [A], eng='act')
                pt, ps = kb.ps_big()
                kb.mm(ps[R16, :], cs.t[R16, CS_SELF:CS_SELF + 16], A.t[:, sl], True, True, [cs, A], pt)
                kb.act(B.t[:, sl], ps[R16, :], AF.Sigmoid, pt + [p], [B], bias=p.t[R16, PC_GBF:PC_GBF + 1])
                kb.act(B.t[:, sl], B.t[:, sl], AF.Ln, [B], [B])
                pt, ps = kb.ps_big()
                kb.mm(ps[R16, :], cs.t[R16, CS_SELI:CS_SELI + 16], A.t[:, sl], True, True, [cs, A], pt)
                kb.act(C.t[:, sl], ps[R16, :], AF.Identity, pt + [p], [C], bias=p.t[R16, PC_GBI:PC_GBI + 1])
            self.make_rmask(256)
            for d in range(2):
                self.cumsum_chunks(A, B, d, 16)
                Av = A.t[:, :].rearrange("p (a b) -> p a b", b=256)
                gt_ = Av[:, :, 255:256] if d == 0 else Av[:, :, 0:1]
                kb.cp(sm[d].t[:, :].rearrange("p (a b) -> p a b", b=1), gt_, [A], [sm[d]], eng='dve')
                kb.tt(A[:], C[:], A[:], ALU.subtract, [C, A], [A])
                kb.op('dve', lambda e: e.tensor_reduce(out=sm[2 + d][:], in_=Av, axis=AX.X, op=ALU.max), [A], [sm[2 + d]])
            kb.ts(sm[0][:], sm[0][:], mdF, None, ALU.mult, None, [sm[0], cs], [sm[0]])
            kb.stt(sm[0][:], sm[1][:], mdB, sm[0][:], ALU.mult, ALU.add, [sm[1], cs, sm[0]], [sm[0]])
            kb.ts(sm[2][:], sm[2][:], mdF, None, ALU.mult, None, [sm[2], cs], [sm[2]])
            kb.stt(sm[2][:], sm[3][:], mdB, sm[2][:], ALU.mult, ALU.add, [sm[3], cs, sm[2]], [sm[2]])
            kb.ts(sm[2][:], sm[2][:], 0.0, None, ALU.max, None, [sm[2]], [sm[2]])
            kb.tt(mfin[:], sm[0][:], sm[2][:], ALU.add, [sm[0], sm[2]], [mfin])
            kb.out_toks.append(kb.dma('sp', self.so_m[l], mfin[:], reads=[mfin]))
            self.make_rmask(32)
            self.cumsum_chunks(A, B, 0, 16)
            self.cumsum_chunks(Dd, B, 1, 16)
            kb.ts(A[:], A[:], mdF, None, ALU.mult, None, [A, cs], [A])
            kb.stt(A[:], Dd[:], mdB, A[:], ALU.mult, ALU.add, [Dd, cs, A], [A])
            kb.stt(Z[:], A[:], sgn, C[:], ALU.mult, ALU.add, [A, cs, C], [Z])
            kb.barrier()
        QT = [kb.sb("m_QT%d" % d, [128, TOK], BF16, st) for d in range(2)]
        KT = [kb.sb("m_KT%d" % d, [128, TOK], BF16, st) for d in range(2)]
        KTe = [kb.sb("m_KTe%d" % d, [128, TOK], BF16, st) for d in range(2)]
        vi = kb.sb("m_vi", [128, TB], BF16, st)
        Vtm = kb.sb("m_Vtm", [128, NTT, 130], BF16, st)
        osig = kb.sb("m_osig", [128, TOK], BF16, st)
        oacc = [kb.sb("m_oacc%d" % tb, [128, TB], F32, st) for tb in range(NTB)]
        sr = kb.sb("m_sr", [16, 2, 128], F32, st)
        emf = [kb.sb("m_emf%d" % d, [128, NSEG], F32, st) for d in range(2)]
        em0 = kb.sb("m_em0", [128, 1], F32, st)
        stage = [kb.sb("m_stage%d" % i, [128, 129], F32, st) for i in range(2)]
        G = {'KT': KT, 'QT': QT, 'KTe': KTe, 'Vtm': Vtm, 'oacc': oacc, 'NV': 129, 'den': True,
             'S32': [[kb.sb("m_S32_%d_%d" % (d, i), [128, 129], F32, st) for i in range(2)] for d in range(2)],
             'e1': [kb.sb("m_e1_%d" % d, [128, 64], F32, st) for d in range(2)],
             'Ktm': [[kb.sb("m_Ktm%d_%d" % (d, i), [128, 128], BF16, st) for i in range(2)] for d in range(2)],
             'ATs': [[kb.sb("m_ATs%d_%d" % (d, i), [128, 128], BF16, st) for i in range(2)] for d in range(2)],
             'mask': [self.cst.t[:, CS_M128F:CS_M128F + 128], self.cst.t[:, CS_M128B:CS_M128B + 128]]}
        kb.memset(Vtm.t[:, :, 128:129], 1.0, [Vtm])
        stg_i = [0]
        mpre = None
        for hd in range(4):
            with ExitStack() as sprep:
                qT = kb.sb("m_qT", [128, TOK], BF16, sprep)
                kT = kb.sb("m_kT", [128, TOK], BF16, sprep)
                eg = [kb.sb("m_eg%d" % i, [128, TB], F32, sprep) for i in range(2)]
                if mpre is not None:
                    wt1 = mpre
                else:
                    wt1 = self.w_in_cols(l, [C_MQ + hd * 128, C_MK + hd * 128, C_MV + hd * 128, C_MO + hd * 128])
                mpre = None
                for tb in range(NTB):
                    sl = slice(tb * TB, (tb + 1) * TB)
                    pt, ps = self.proj(wt1, 0, tb)
                    kb.cp(qT.t[:, sl], ps, pt, [qT], eng='act')
                    pt, ps = self.proj(wt1, 1, tb)
                    kb.act(kT.t[:, sl], ps, AF.Copy, pt, [kT], scale=float(128 ** -0.5))
                    pt, ps = self.proj(wt1, 2, tb)
                    kb.cp(vi[:], ps, pt, [vi], eng='dve')
                    for ti in range(4):
                        self.to_tm(Vtm, Vtm.t[:, tb * 4 + ti, 0:128], vi, vi.t[:, ti * 128:(ti + 1) * 128])
                    pt, ps = self.proj(wt1, 3, tb)
                    kb.act(osig.t[:, sl], ps, AF.Sigmoid, pt, [osig])
                for d in range(2):
                    p_ = d * 4 + hd
                    kb.ts(sr.t[:, 0, :], cs.t[R16, CS_ONES:CS_ONES + 128], cs.t[R16, CS_ID + p_:CS_ID + p_ + 1], None, ALU.mult, None, [cs], [sr])
                    kb.ts(sr.t[:, 1, :], cs.t[R16, CS_ONES:CS_ONES + 128], cs.t[R16, CS_ID + 8 + p_:CS_ID + 9 + p_], None, ALU.mult, None, [cs], [sr])
                    for tb in range(NTB):
                        sl = slice(tb * TB, (tb + 1) * TB)
                        pt, ps = kb.ps_big()
                        kb.mm(ps, sr.t[:, 0, :], Z.t[:, sl], True, True, [sr, Z], pt)
                        e_ = eg[0]
                        kb.act(e_[:], ps, AF.Exp, pt, [e_])
                        kb.tt(QT[d].t[:, sl], qT.t[:, sl], e_[:], ALU.mult, [qT, e_], [QT[d]])
                        ev = e_.t[:, :].rearrange("p (a b) -> p a b", b=32)
                        kb.cp(G['e1'][d].t[:, tb * 16:(tb + 1) * 16].rearrange("p (a b) -> p a b", b=1), ev[:, :, 31:32] if d == 0 else ev[:, :, 0:1],
                              [e_], [G['e1'][d]], eng='dve')
                        pt, ps = kb.ps_big()
                        kb.mm(ps, sr.t[:, 1, :], Z.t[:, sl], True, True, [sr, Z], pt)
                        e2_ = eg[1]
                        kb.act(e2_[:], ps, AF.Exp, pt, [e2_])
                        kb.tt(KT[d].t[:, sl], kT.t[:, sl], e2_[:], ALU.mult, [kT, e2_], [KT[d]])
                    kb.tt(KTe[d].t[:, :].rearrange("p (a b) -> p a b", b=32), KT[d].t[:, :].rearrange("p (a b) -> p a b", b=32),
                          G['e1'][d].t[:, :].rearrange("p (a b) -> p a b", b=1).to_broadcast([128, 64, 32]), ALU.mult,
                          [KT[d], G['e1'][d]], [KTe[d]])
                    pt, ps = kb.ps_small()
                    kb.mm(ps[:, 0:NSEG], sr.t[:, 1, :], mfin[:], True, True, [sr, mfin], pt)
                    kb.act(emf[d][:], ps[:, 0:NSEG], AF.Exp, pt, [emf[d]], scale=-1.0)
                    s0 = G['S32'][d][0]
                    kb.dma('sp', s0.t[:, 0:128], self.cs0[l, d, hd], writes=[s0])
                    kb.dma('sp', s0.t[:, 128:129], self.ns0[l, d, hd], writes=[s0])
                    mcol = self.cc.t[:, CC_M0 + l * 8 + d * 4 + hd:CC_M0 + l * 8 + d * 4 + hd + 1]
                    kb.act(em0[:], mcol, AF.Exp, [self.cc], [em0])
                    kb.ts(s0[:], s0[:], em0[:], None, ALU.mult, None, [s0, em0], [s0])
                kb.barrier()
            if hd < 3:
                n_ = hd + 1
                mpre = self.w_in_cols(l, [C_MQ + n_ * 128, C_MK + n_ * 128, C_MV + n_ * 128, C_MO + n_ * 128])
            for tb in range(NTB):
                kb.memset(oacc[tb][:], 0.0, [oacc[tb]])
            G['cur'] = [0, 0]

            def seg_out(d, seg, s32, hd=hd):
                sg_ = stage[stg_i[0]]
                stg_i[0] ^= 1
                kb.act(sg_[:], s32[:], AF.Copy, [s32, emf[d]], [sg_], scale=emf[d].t[:, seg:seg + 1])
                kb.out_toks.append(kb.dma('sp', self.so_c[l, d, seg, hd], sg_.t[:, 0:128], reads=[sg_]))
                kb.out_toks.append(kb.dma('sp', self.so_n[l, d, seg, hd], sg_.t[:, 128:129], reads=[sg_]))
            G['seg_out'] = seg_out
            with ExitStack() as sg_:
                self.gla_run(G, sg_)
                kb.barrier()
            self.head_post(l, oacc, p.t[:, PC_MNORM + hd:PC_MNORM + hd + 1], osig, self.ycur[hd], vi)

    def shift_chunk(self, l, cidx, raw, sh, tmpb, nxt=None):
        kb = self.kb
        me = self.mue[l]
        pre = getattr(self, '_shift_pre', None)
        if pre is not None and pre[0] == (l, cidx):
            wt = pre[1]
        else:
            wt = self.w_in_cols(l, [C_RW + cidx * 128])
        self._shift_pre = None
        if nxt is not None:
            self._shift_pre = ((l, nxt), self.w_in_cols(l, [C_RW + nxt * 128]))
        for tb in range(NTB):
            pt, ps = self.proj(wt, 0, tb)
            kb.cp(raw.t[:, tb * TB:(tb + 1) * TB], ps, pt, [raw], eng='act' if tb % 2 else 'dve')
        mu = [me.t[:, j * 15 + cidx:j * 15 + cidx + 1] for j in range(4)]
        c0 = me.t[:, 60 + cidx:61 + cidx]
        kb.act(sh[:], raw[:], AF.Copy, [raw, me], [sh], scale=c0)
        rv = raw.t[:, :].rearrange("p (a b) -> p a b", b=64)
        sv = sh.t[:, :].rearrange("p (a b) -> p a b", b=64)
        R, W = [raw, me, sh], [sh]
        kb.stt(sv[:, :, 1:64], rv[:, :, 0:63], mu[0], sv[:, :, 1:64], ALU.mult, ALU.add, R, W)
        kb.stt(sv[:, :, 0:63], rv[:, :, 1:64], mu[1], sv[:, :, 0:63], ALU.mult, ALU.add, R, W)
        kb.stt(sh.t[:, 64:TOK], raw.t[:, 0:TOK - 64], mu[2], sh.t[:, 64:TOK], ALU.mult, ALU.add, R, W)
        kb.stt(sh.t[:, 0:TOK - 64], raw.t[:, 64:TOK], mu[3], sh.t[:, 0:TOK - 64], ALU.mult, ALU.add, R, W)
        tb3 = tmpb.t[:, :].rearrange("p (a b) -> p a b", b=1)
        bml = self.cc.t[:, CC_BML:CC_BML + 31].rearrange("p (a b) -> p a b", b=1)
        bmr = self.cc.t[:, CC_BMR:CC_BMR + 31].rearrange("p (a b) -> p a b", b=1)
        kb.tt(tb3, rv[:, 0:31, 63:64], bml, ALU.mult, [raw, self.cc], [tmpb])
        kb.stt(sv[:, 1:32, 0:1], tb3, mu[0], sv[:, 1:32, 0:1], ALU.mult, ALU.add, [tmpb, me, sh], W)
        kb.tt(tb3, rv[:, 1:32, 0:1], bmr, ALU.mult, [raw, self.cc], [tmpb])
        kb.stt(sv[:, 0:31, 63:64], tb3, mu[1], sv[:, 0:31, 63:64], ALU.mult, ALU.add, [tmpb, me, sh], W)

    def branch_rwkv(self, l, st):
        kb = self.kb
        c, p, cs = self.cols[l], self.pcl[l], self.cst
        self.make_rmask(128)
        twT = kb.sb("r_twT", [128, TOK], BF16, st)
        aloT = kb.sb("r_aloT", [128, TOK], BF16, st)
        sglo = kb.sb("r_sglo", [128, TOK], BF16, st)
        W2 = kb.sb("r_W2", [128, 512], BF16, st)
        A2 = kb.sb("r_A2", [128, 512], BF16, st)
        G2 = kb.sb("r_G2", [128, 512], BF16, st)
        omka = kb.sb("r_omka", [128, 4], F32, st)
        tmpb = kb.sb("r_tmpb", [128, 31], F32, st)
        for d in range(2):
            kb.dma('pool', W2.t[64 * d:64 * d + 64, :], self.w2[l, d], writes=[W2])
            kb.dma('pool', A2.t[64 * d:64 * d + 64, :], self.a2[l, d], writes=[A2])
        kb.dma('pool', G2[:], self.g2[l], writes=[G2])
        kb.ts(omka[:], p.t[:, PC_KA:PC_KA + 4], -1.0, 1.0, ALU.mult, ALU.add, [p], [omka])
        with ExitStack() as s0:
            raw = kb.sb("r_raw", [128, TOK], F32, s0)
            sh = kb.sb("r_sh", [128, TOK], F32, s0)
            self.shift_chunk(l, 12, raw, sh, tmpb, nxt=13)
            kb.act(twT[:], sh[:], AF.Tanh, [sh], [twT])
            self.shift_chunk(l, 13, raw, sh, tmpb, nxt=14)
            kb.cp(aloT[:], sh[:], [sh], [aloT], eng='act')
            self.shift_chunk(l, 14, raw, sh, tmpb, nxt=0)
            kb.act(sglo[:], sh[:], AF.Sigmoid, [sh], [sglo])
            kb.barrier()
        rwm = [cs.t[:, CS_RWF:CS_RWF + 256], cs.t[:, CS_RWB:CS_RWB + 256]]
        strict_st = [cs.t[:, CS_RWF:CS_RWF + 128], cs.t[:, CS_RWB:CS_RWB + 128]]
        strict_ts = [cs.t[:, CS_RWB:CS_RWB + 128], cs.t[:, CS_RWF:CS_RWF + 128]]
        ident_b = self.ident_b
        for pr in range(4):
            with ExitStack() as sp:
                rT = kb.sb("r_rT", [128, TOK], BF16, sp)
                kT = kb.sb("r_kT", [128, TOK], BF16, sp)
                kkT = kb.sb("r_kkT", [128, TOK], BF16, sp)
                Vtm = kb.sb("r_Vtm", [128, NTT, 128], BF16, sp)
                bon = kb.sb("r_bon", [128, TOK], BF16, sp)
                oacc = [kb.sb("r_oacc%d" % tb, [128, TB], F32, sp) for tb in range(NTB)]
                kkc = p.t[:, PC_KK + pr:PC_KK + pr + 1]
                rkc = p.t[:, PC_RK + pr:PC_RK + pr + 1]
                kac = p.t[:, PC_KA + pr:PC_KA + pr + 1]
                omkac = omka.t[:, pr:pr + 1]
                with ExitStack() as s1:
                    raw = kb.sb("r_raw", [128, TOK], F32, s1)
                    sh = kb.sb("r_sh", [128, TOK], F32, s1)
                    vT = kb.sb("r_vT", [128, TOK], BF16, s1)
                    t5 = [kb.sb("r_t5_%d" % i, [128, TB], F32, s1) for i in range(2)]
                    b5 = [kb.sb("r_b5_%d" % i, [128, TB], BF16, s1) for i in range(2)]
                    self.shift_chunk(l, pr, raw, sh, tmpb, nxt=4 + pr)
                    kb.cp(rT[:], sh[:], [sh], [rT], eng='act')
                    self.shift_chunk(l, 4 + pr, raw, sh, tmpb, nxt=8 + pr)
                    kb.cp(kT[:], sh[:], [sh], [kT], eng='act')
                    for tb in range(NTB):
                        sl = slice(tb * TB, (tb + 1) * TB)
                        kr = t5[tb % 2]
                        kb.ts(kr[:], sh.t[:, sl], kkc, None, ALU.mult, None, [sh, p], [kr])
                        sq = b5[tb % 2]
                        kb.act(sq[:], kr[:], AF.Square, [kr], [sq])
                        pt, ps = kb.ps_big()
                        kb.mm(ps, self.bones_b, sq[:], True, True, [self.cstb, sq], pt)
                        nr = self.rstd[tb % 2]
                        kb.act(nr[:], ps, AF.Sqrt, pt, [nr])
                        kb.ts(nr[:], nr[:], 1e-12, None, ALU.max, None, [nr], [nr])
                        kb.op('dve', lambda e: e.reciprocal(out=nr[:], in_=nr[:]), [nr], [nr])
                        kb.tt(kkT.t[:, sl], kr[:], nr[:], ALU.mult, [kr, nr], [kkT])
                    self.shift_chunk(l, 8 + pr, raw, sh, tmpb, nxt=(pr + 1) if pr < 3 else None)
                    kb.cp(vT[:], sh[:], [sh], [vT], eng='act')
                    for ti in range(NTT):
                        self.to_tm(Vtm, Vtm.t[:, ti, :], vT, vT.t[:, ti * 128:(ti + 1) * 128], eng='act' if ti % 2 else 'dve')
                    for tb in range(NTB):
                        sl = slice(tb * TB, (tb + 1) * TB)
                        bk = b5[tb % 2]
                        kb.stt(bk[:], rT.t[:, sl], rkc, kT.t[:, sl], ALU.mult, ALU.mult, [rT, p, kT], [bk])
                        pt, ps = kb.ps_big()
                        kb.mm(ps, self.bones_b, bk[:], True, True, [self.cstb, bk], pt)
                        kb.tt(bon.t[:, sl], ps, vT.t[:, sl], ALU.mult, pt + [vT], [bon])
                    kb.barrier()
                for tb in range(NTB):
                    kb.memset(oacc[tb][:], 0.0, [oacc[tb]])
                for d in range(2):
                    with ExitStack() as sd:
                        AR = kb.sb("r_AR", [128, NTT, 2, 128], BF16, sd)
                        BT = kb.sb("r_BT", [128, TOK], BF16, sd)
                        KTt = kb.sb("r_KTt", [128, TOK], BF16, sd)
                        e1 = kb.sb("r_e1", [128, NTT], F32, sd)
                        e2 = kb.sb("r_e2", [128, NTT], F32, sd)
                        e3 = kb.sb("r_e3", [128, NTT], F32, sd)
                        w0c = p.t[:, PC_W0 + 4 * d + pr:PC_W0 + 4 * d + pr + 1]
                        a0c = p.t[:, PC_A0 + 4 * d + pr:PC_A0 + 4 * d + pr + 1]
                        R64 = slice(64 * d, 64 * d + 64)
                        with ExitStack() as s2:
                            sw = kb.sb("r_sw", [128, TB], F32, s2)
                            cw = kb.sb("r_cw", [128, TB], F32, s2)
                            at = kb.sb("r_at", [128, TB], F32, s2)
                            km = kb.sb("r_km", [128, TB], F32, s2)
                            kd = kb.sb("r_kd", [128, TB], BF16, s2)
                            bt = kb.sb("r_bt", [128, TB], BF16, s2)
                            et = kb.sb("r_et", [128, TB], BF16, s2)
                            cmid = kb.sb("r_cmid", [128, 4], F32, s2)
                            ctmp = kb.sb("r_ctmp", [128, 4], F32, s2)
                            for tb in range(NTB):
                                sl = slice(tb * TB, (tb + 1) * TB)
                                csl = slice(tb * 4, tb * 4 + 4)
                                pt, ps = kb.ps_big()
                                kb.mm(ps, W2.t[R64, pr * 128:(pr + 1) * 128], twT.t[R64, sl], True, True, [W2, twT], pt)
                                kb.act(sw[:], ps, AF.Sigmoid, pt + [p], [sw], bias=w0c)
                                pt, ps = kb.ps_big()
                                kb.mm(ps, A2.t[R64, pr * 128:(pr + 1) * 128], aloT.t[R64, sl], True, True, [A2, aloT], pt)
                                kb.act(at[:], ps, AF.Sigmoid, pt + [p], [at], bias=a0c)
                                kb.ts(km[:], at[:], kac, omkac, ALU.mult, ALU.add, [at, p, omka], [km])
                                kb.tt(kd[:], kT.t[:, sl], km[:], ALU.mult, [kT, km], [kd])
                                kb.tt(bt[:], kkT.t[:, sl], at[:], ALU.mult, [kkT, at], [bt])
                                if d == 0:
                                    kb.op('dve', lambda e: e.tensor_tensor_scan(out=cw[:], data0=self.rmF.t[:, :], data1=sw[:], initial=0.0,
                                                                                op0=ALU.mult, op1=ALU.add), [self.rmF, sw], [cw])
                                else:
                                    kb.op('dve', lambda e: e.tensor_tensor_scan(out=cw.t[:, ::-1], data0=self.rmB.t[:, ::-1], data1=sw.t[:, ::-1],
                                                                                initial=0.0, op0=ALU.mult, op1=ALU.add), [self.rmB, sw], [cw])
                                cv = cw.t[:, :].rearrange("p (a b) -> p a b", b=128)
                                cend = cv[:, :, 127:128] if d == 0 else cv[:, :, 0:1]
                                cm3 = cmid.t[:, :].rearrange("p (a b) -> p a b", b=1)
                                ct3 = ctmp.t[:, :].rearrange("p (a b) -> p a b", b=1)
                                kb.cp(cm3, cv[:, :, 63:64], [cw], [cmid], eng='dve')
                                kb.act(e1.t[:, csl].rearrange("p (a b) -> p a b", b=1), cend, AF.Exp, [cw], [e1], scale=-KAPPA)
                                kb.act(e3.t[:, csl], cmid[:], AF.Exp, [cmid], [e3], scale=-KAPPA)
                                kb.tt(ct3, cend, cm3, ALU.subtract, [cw, cmid], [ctmp])
                                kb.act(e2.t[:, csl], ctmp[:], AF.Exp, [ctmp], [e2], scale=-KAPPA)
                                kb.tt(sw[:], cw[:], sw[:], ALU.subtract, [cw, sw], [sw])
                                swv = sw.t[:, :].rearrange("p (a b) -> p a b", b=128)
                                kb.tt(swv, swv, cm3.to_broadcast([128, 4, 128]), ALU.subtract, [sw, cmid], [sw])
                                kb.tt(cv, cv, cm3.to_broadcast([128, 4, 128]), ALU.subtract, [cw, cmid], [cw])
                                arv = AR.t[:, tb * 4:(tb + 1) * 4, :, :]
                                kb.act(et[:], cw[:], AF.Exp, [cw], [et], scale=-KAPPA)
                                kb.tt(arv[:, :, 1, :], rT.t[:, sl].rearrange("p (a b) -> p a b", b=128), et.t[:, :].rearrange("p (a b) -> p a b", b=128),
                                      ALU.mult, [rT, et], [AR])
                                kb.act(et[:], cw[:], AF.Exp, [cw], [et], scale=KAPPA)
                                kb.tt(BT.t[:, sl], bt[:], et[:], ALU.mult, [bt, et], [BT])
                                kb.tt(KTt.t[:, sl], kd[:], et[:], ALU.mult, [kd, et], [KTt])
                                kb.act(et[:], sw[:], AF.Exp, [sw], [et], scale=-KAPPA)
                                kb.tt(arv[:, :, 0, :], kkT.t[:, sl].rearrange("p (a b) -> p a b", b=128), et.t[:, :].rearrange("p (a b) -> p a b", b=128),
                                      ALU.mult, [kkT, et], [AR])
                            kb.barrier()
                        self.rwkv_chain(l, d, pr, AR, BT, KTt, e1, e2, e3, Vtm, oacc, rwm[d], strict_st[d], strict_ts[d], sd)
                        kb.barrier()
                with ExitStack() as s3:
                    ob = kb.sb("r_ob", [128, 2, TB], BF16, s3)
                    mean = kb.sb("r_mean", [128, TB], F32, s3)
                    var = kb.sb("r_var", [128, TB], F32, s3)
                    cen = kb.sb("r_cen", [128, TB], F32, s3)
                    ln0 = p.t[:, PC_LN0 + pr:PC_LN0 + pr + 1]
                    ln1 = p.t[:, PC_LN1 + pr:PC_LN1 + pr + 1]
                    for tb in range(NTB):
                        sl = slice(tb * TB, (tb + 1) * TB)
                        o = oacc[tb]
                        kb.cp(ob.t[:, 0, :], o[:], [o], [ob], eng='act')
                        kb.act(ob.t[:, 1, :], o[:], AF.Square, [o], [ob])
                        ptm, psm = kb.ps_big()
                        kb.mm(psm, self.bones_b, ob.t[:, 0, :], True, True, [self.cstb, ob], ptm)
                        pts, pss = kb.ps_big()
                        kb.mm(pss, self.bones_b, ob.t[:, 1, :], True, True, [self.cstb, ob], pts)
                        kb.act(mean[:], psm, AF.Copy, ptm, [mean], scale=1.0 / 64)
                        kb.tt(var[:], mean[:], mean[:], ALU.mult, [mean], [var])
                        kb.stt(var[:], pss, 1.0 / 64, var[:], ALU.mult, ALU.subtract, pts + [var], [var])
                        kb.act(var[:], var[:], AF.Sqrt, [var], [var], bias=64e-5, scale=1.0)
                        kb.op('dve', lambda e: e.reciprocal(out=var[:], in_=var[:]), [var], [var])
                        kb.tt(cen[:], o[:], mean[:], ALU.subtract, [o, mean], [cen])
                        kb.stt(cen[:], cen[:], ln0, var[:], ALU.mult, ALU.mult, [cen, p, var], [cen])
                        kb.stt(cen[:], cen[:], ln1, bon.t[:, sl], ALU.add, ALU.add, [cen, p, bon], [cen])
                        ptg, psg = kb.ps_big()
                        kb.mm(psg, G2.t[:, pr * 128:(pr + 1) * 128], sglo.t[:, sl], True, True, [G2, sglo], ptg)
                        kb.tt(self.ycur[pr].t[:, sl], cen[:], psg, ALU.mult, [cen] + ptg, [self.ycur[pr]])
                    kb.barrier()

    def rwkv_chain(self, l, d, pr, AR, BT, KTt, e1, e2, e3, Vtm, oacc, rwm, m_st, m_ts, st):
        kb = self.kb
        ident_b = self.ident_b
        NPB = 3
        NA = [[kb.sb("r_NA%d_%d" % (hp, i), [128, 128], BF16, st) for i in range(NPB)] for hp in range(2)]
        NK = [[kb.sb("r_NK%d_%d" % (hp, i), [128, 256], BF16, st) for i in range(NPB)] for hp in range(2)]
        TT = [[kb.sb("r_TT%d_%d" % (hp, i), [128, 128], BF16, st) for i in range(NPB)] for hp in range(2)]
        Xb = [[[kb.sb("r_X%d_%d_%d" % (hp, i, u), [128, 128], BF16, st) for i in range(2)] for hp in range(2)] for u in range(NPB)]
        XTb = [[[kb.sb("r_XT%d_%d_%d" % (hp, i, u), [128, 128], BF16, st) for i in range(2)] for hp in range(2)] for u in range(NPB)]
        Btm = [kb.sb("r_Btm%d" % i, [128, 128], BF16, st) for i in range(NPB)]
        Ktm = [kb.sb("r_Ktm%d" % i, [128, 128], BF16, st) for i in range(NPB)]
        Mz = [kb.sb("r_Mz%d" % hp, [128, 64], BF16, st) for hp in range(2)]
        nW = [kb.sb("r_nW%d" % hp, [128, 64], BF16, st) for hp in range(2)]
        U = [kb.sb("r_U%d" % hp, [128, 64], BF16, st) for hp in range(2)]
        M32 = [kb.sb("r_M32_%d" % i, [128, 64], F32, st) for i in range(2)]
        Pe = kb.sb("r_Pe", [128, 64], F32, st)
        for hp in range(2):
            kb.memset(Mz[hp][:], 0.0, [Mz[hp]])
        cur = 0
        for hp in range(2):
            kb.dma('sp', M32[0].t[64 * hp:64 * hp + 64, :], self.rs0[l, d, 2 * pr + hp], writes=[M32[0]])
        order = list(range(NTT)) if d == 0 else list(range(NTT - 1, -1, -1))

        def par(tile, bi):
            tsl = slice(tile * 128, (tile + 1) * 128)
            self.to_tm(Btm[bi], Btm[bi][:], BT, BT.t[:, tsl], eng='act')
            self.to_tm(Ktm[bi], Ktm[bi][:], KTt, KTt.t[:, tsl], eng='dve')
            yield
            pss_ = []
            for hp in range(2):
                H = slice(64 * hp, 64 * hp + 64)
                arf = AR.t[H, tile, :, :].rearrange("p a b -> p (a b)")
                pt1, ps1 = kb.ps_small()
                kb.mm(ps1, BT.t[H, tsl], arf, True, True, [BT, AR], pt1)
                pt2, ps2 = kb.ps_small()
                kb.mm(ps2, KTt.t[H, tsl], arf, True, True, [KTt, AR], pt2)
                pt3, ps3 = kb.ps_small()
                kb.mm(ps3[:, 0:128], AR.t[H, tile, 0, :], BT.t[H, tsl], True, True, [AR, BT], pt3)
                pss_.append((pt1, ps1, pt2, ps2, pt3, ps3))
            yield
            cur_x = []
            for hp in range(2):
                pt1, ps1, pt2, ps2, pt3, ps3 = pss_[hp]
                na, nk, tt_ = NA[hp][bi], NK[hp][bi], TT[hp][bi]
                X, XT = Xb[bi][hp][0], XTb[bi][hp][0]
                kb.stt(X[:], ps3[:, 0:128], -1.0, m_ts, ALU.mult, ALU.mult, pt3 + [self.cst], [X])
                kb.stt(XT[:], ps1[:, 0:128], -1.0, m_st, ALU.mult, ALU.mult, pt1 + [self.cst], [XT])
                kb.tt(na[:], ps1[:, 128:256], rwm[:, 128:256], ALU.mult, pt1 + [self.cst], [na])
                kb.tt(nk[:], ps2, rwm, ALU.mult, pt2 + [self.cst], [nk])
                kb.tt(tt_[:], XT[:], ident_b, ALU.add, [XT, self.cstb], [tt_], eng='pool')
                cur_x.append((X, XT))
            yield
            pend = [None, None]
            for r in range(6):
                pp = []
                for hp in range(2):
                    X, XT = cur_x[hp]
                    tt_ = TT[hp][bi]
                    pa, psa = kb.ps_small()
                    kb.mm(psa[:, 0:128], XT[:], X[:], True, True, [XT, X], pa)
                    pb = psb = None
                    if r < 5:
                        pb, psb = kb.ps_small()
                        kb.mm(psb[:, 0:128], X[:], XT[:], True, True, [X, XT], pb)
                    pp.append((pa, psa, pb, psb))
                yield
                for hp in range(2):
                    pa, psa, pb, psb = pp[hp]
                    X2, XT2 = Xb[bi][hp][(r + 1) % 2], XTb[bi][hp][(r + 1) % 2]
                    kb.cp(X2[:], psa[:, 0:128], pa, [X2], eng='act')
                    if r < 5:
                        kb.cp(XT2[:], psb[:, 0:128], pb, [XT2], eng='dve')
                    cur_x[hp] = (X2, XT2)
                yield
                pcs = []
                for hp in range(2):
                    X2 = cur_x[hp][0]
                    tt_ = TT[hp][bi]
                    pc_, psc = kb.ps_small()
                    kb.mm(psc[:, 0:128], X2[:], tt_[:], True, True, [X2, tt_], pc_)
                    pcs.append((pc_, psc))
                yield
                for hp in range(2):
                    tt_ = TT[hp][bi]
                    pc_, psc = pcs[hp]
                    kb.tt(tt_[:], tt_[:], psc[:, 0:128], ALU.add, [tt_] + pc_, [tt_])
                yield

        def seq(tile, bi):
            nonlocal cur
            m32 = M32[cur]
            for hp in range(2):
                H = slice(64 * hp, 64 * hp + 64)
                kb.act(Mz[hp].t[H, :], m32.t[H, :], AF.Copy, [m32, e3], [Mz[hp]], scale=e3.t[H, tile:tile + 1])
            yield
            pw = []
            for hp in range(2):
                V_h = Vtm.t[:, tile, 64 * hp:64 * hp + 64]
                pt, ps = kb.ps_small()
                kb.mm(ps[:, 0:64], AR.t[:, tile, 0, :], Mz[hp][:], True, False, [AR, Mz[hp]], pt)
                kb.mm(ps[:, 0:64], NK[hp][bi].t[:, 0:128], V_h, False, True, [NK[hp][bi], Vtm], pt)
                pw.append((pt, ps))
            yield
            for hp in range(2):
                pt, ps = pw[hp]
                kb.act(nW[hp][:], ps[:, 0:64], AF.Copy, pt, [nW[hp]], scale=-1.0)
            yield
            pu = []
            for hp in range(2):
                pt, ps = kb.ps_small()
                kb.mm(ps[:, 0:64], TT[hp][bi][:], nW[hp][:], True, True, [TT[hp][bi], nW[hp]], pt)
                pu.append((pt, ps))
            yield
            for hp in range(2):
                pt, ps = pu[hp]
                kb.cp(U[hp][:], ps[:, 0:64], pt, [U[hp]], eng='dve')
            yield
            pts, pss = kb.ps_small()
            for hp in range(2):
                H = slice(64 * hp, 64 * hp + 64)
                V_h = Vtm.t[:, tile, 64 * hp:64 * hp + 64]
                kb.mm(pss[H, 0:64], Btm[bi].t[:, H], U[hp][:], True, False, [Btm[bi], U[hp]], pts)
                kb.mm(pss[H, 0:64], Ktm[bi].t[:, H], V_h, False, True, [Ktm[bi], Vtm], pts)
            pto, pso = kb.ps_small()
            for hp in range(2):
                H = slice(64 * hp, 64 * hp + 64)
                V_h = Vtm.t[:, tile, 64 * hp:64 * hp + 64]
                kb.mm(pso[H, 0:128], Mz[hp][:], AR.t[:, tile, 1, :], True, False, [Mz[hp], AR], pto)
                kb.mm(pso[H, 0:128], U[hp][:], NA[hp][bi].t[:, 0:128], False, False, [U[hp], NA[hp][bi]], pto)
                kb.mm(pso[H, 0:128], V_h, NK[hp][bi].t[:, 128:256], False, True, [Vtm, NK[hp][bi]], pto)
            yield
            kb.act(Pe[:], pss[:, 0:64], AF.Copy, pts + [e2], [Pe], scale=e2.t[:, tile:tile + 1])
            kb.stt(m32[:], m32[:], e1.t[:, tile:tile + 1], Pe[:], ALU.mult, ALU.add, [m32, e1, Pe], [m32])
            ob = oacc[tile // 4]
            osl = slice((tile % 4) * 128, (tile % 4) * 128 + 128)
            kb.tt(ob.t[:, osl], ob.t[:, osl], pso[:, 0:128], ALU.add, [ob] + pto, [ob])
            seg_end = (d == 0 and tile % 2 == 1) or (d == 1 and tile % 2 == 0)
            if seg_end:
                seg = tile // 2
                for hp in range(2):
                    kb.out_toks.append(kb.dma('sp', self.so_r[l, d, seg, 2 * pr + hp], m32.t[64 * hp:64 * hp + 64, :], reads=[m32]))
                nxt = M32[1 - cur]
                kb.ts(nxt[:], m32[:], self.cc.t[:, CC_KEEP:CC_KEEP + 1], None, ALU.mult, None, [m32, self.cc], [nxt])
                cur = 1 - cur
            yield

        def step(g):
            try:
                next(g)
                return True
            except StopIteration:
                return False

        pars = {}
        for j in range(min(NPB - 1, NTT)):
            pars[j] = par(order[j], j % NPB)
        nxt_par = min(NPB - 1, NTT)
        for i, tile in enumerate(order):
            if i in pars:
                while step(pars[i]):
                    pass
                del pars[i]
            if nxt_par < NTT:
                pars[nxt_par] = par(order[nxt_par], nxt_par % NPB)
                nxt_par += 1
            g = seq(tile, i % NPB)
            alive = True
            while alive:
                alive = step(g)
                for j in sorted(pars):
                    for _ in range(2):
                        if not step(pars[j]):
                            del pars[j]
                            break

    def merge(self, l, i, first):
        kb = self.kb
        wb = self.wbt
        kb.dma('pool', wb.t[:, :].rearrange("p (c m) -> p c m", m=1024), self.w_branch[l, i].rearrange("(c p) m -> p c m", p=128), writes=[wb])
        wbv = wb.t[:, :].rearrange("p (c m) -> p c m", m=1024)
        for half in range(2):
            wg = self.w_in_cols(l, [C_GATES + i * 1024 + half * 512 + j * 128 for j in range(4)])
            for mm_ in range(4):
                m = half * 4 + mm_
                for tb in range(NTB):
                    ptg, pg = self.proj(wg, mm_, tb)
                    gate = self.rstd[(m * NTB + tb) % 2]
                    kb.act(gate[:], pg, AF.Sigmoid, ptg, [gate])
                    pt, ps = kb.ps_big()
                    for cch in range(4):
                        kb.mm(ps, wbv[:, cch, m * 128:(m + 1) * 128], self.ycur[cch].t[:, tb * TB:(tb + 1) * TB], cch == 0, cch == 3,
                              [wb, self.ycur[cch]], pt)
                    mg = self.merged[tb]
                    if first:
                        kb.tt(mg.t[:, m, :], ps, gate[:], ALU.mult, pt + [gate], [mg])
                    else:
                        tmp = self.tmpA[self.ta_i]
                        self.ta_i ^= 1
                        kb.tt(tmp[:], ps, gate[:], ALU.mult, pt + [gate], [tmp])
                        kb.tt(mg.t[:, m, :], mg.t[:, m, :], tmp[:], ALU.add, [mg, tmp], [mg], eng='pool')


def _cols(v, n):
    return np.ascontiguousarray(np.asarray(v, np.float32).reshape(n, 128).T)


def _build_consts():
    c = np.zeros((128, NCST), np.float32)
    c[:, CS_ID:CS_ID + 128] = np.eye(128, dtype=np.float32)
    c[:, CS_ONES:CS_ONES + 128] = 1.0
    c[0:64, CS_BONES:CS_BONES + 64] = 1.0
    c[64:128, CS_BONES + 64:CS_BONES + 128] = 1.0
    s = np.arange(128)[:, None]
    t64 = np.arange(64)[None, :]
    c[:, CS_M64F:CS_M64F + 64] = ((s % 64) <= t64)
    c[:, CS_M64B:CS_M64B + 64] = ((s % 64) >= t64)
    t = np.arange(128)[None, :]
    c[:, CS_M128F:CS_M128F + 128] = (s <= t) & ((s // 32) == (t // 32))
    c[:, CS_M128B:CS_M128B + 128] = (s >= t) & ((s // 32) == (t // 32))
    c[:, CS_RWF:CS_RWF + 128] = (s < t)
    c[:, CS_RWF + 128:CS_RWF + 256] = (s <= t)
    c[:, CS_RWB:CS_RWB + 128] = (s > t)
    c[:, CS_RWB + 128:CS_RWB + 256] = (s >= t)
    for j in range(16):
        p = j % 8
        d, h = p // 4, p % 4
        c[(2 * d + 1) * 4 + h, CS_SELF + j] = 1.0
        if j >= 8:
            c[(2 * d) * 4 + h, CS_SELI + j] = 1.0
    c[0:8, CS_SGN] = 1.0
    c[8:16, CS_SGN] = -1.0
    for j in range(16):
        if (j % 8) < 4:
            c[j, CS_MDF] = 1.0
        else:
            c[j, CS_MDB] = 1.0
    return c


def _pack_layer(l, P):
    pc = np.zeros((128, NPC), np.float32)
    for i in range(4):
        pc[:, PC_NW + 8 * i:PC_NW + 8 * i + 8] = _cols(P['norms'][l, i], 8)
    pc[:, PC_BADA:PC_BADA + 48] = _cols(P['b_ada'][l], 48)
    pc[:, PC_LB0:PC_LB0 + 4] = _cols(P['hgrn_lb'][0], 4)
    pc[:, PC_LB1:PC_LB1 + 4] = _cols(P['hgrn_lb'][1], 4)
    pc[:, PC_HNORM:PC_HNORM + 4] = _cols(P['hgrn_norm'][l], 4)
    for j in range(4):
        pc[:, PC_MU + 15 * j:PC_MU + 15 * j + 15] = _cols(P['rwkv_mu'][l, j], 15)
    for d in range(2):
        pc[:, PC_W0 + 4 * d:PC_W0 + 4 * d + 4] = _cols(P['rwkv_w0'][l, d], 4)
        pc[:, PC_A0 + 4 * d:PC_A0 + 4 * d + 4] = _cols(P['rwkv_a0'][l, d], 4)
    pc[:, PC_KK:PC_KK + 4] = _cols(P['rwkv_kk'][l], 4)
    pc[:, PC_KA:PC_KA + 4] = _cols(P['rwkv_ka'][l], 4)
    pc[:, PC_RK:PC_RK + 4] = _cols(P['rwkv_rk'][l], 4)
    pc[:, PC_LN0:PC_LN0 + 4] = _cols(P['rwkv_ln'][l, 0], 4)
    pc[:, PC_LN1:PC_LN1 + 4] = _cols(P['rwkv_ln'][l, 1], 4)
    pc[:, PC_MNORM:PC_MNORM + 4] = _cols(P['mlstm_norm'][l], 4)
    gb = np.asarray(P['mlstm_gate_b'][l], np.float32)
    for j in range(16):
        p = j % 8
        d, h = p // 4, p % 4
        pc[j, PC_GBF] = gb[2 * d + 1, h]
        if j >= 8:
            pc[j, PC_GBI] = gb[2 * d, h]
    return pc


_NC_CACHE = {}


def kernel(**inp):
    P = {k: np.asarray(v) for k, v in inp.items()}
    key = tuple(sorted(DBG.items()))
    if key not in _NC_CACHE:
        prog = Prog()
        prog.build()
        _NC_CACHE[key] = prog
    prog = _NC_CACHE[key]
    nc = prog.nc
    cst = _build_consts()
    pc = np.stack([_pack_layer(l, P) for l in range(DEPTH)], 0)
    f32 = lambda a: np.ascontiguousarray(a, dtype=np.float32)
    shared = {"pc": pc, "cst": cst, "w_ada": f32(P['w_ada']), "w_in": f32(P['w_in']), "rwkv_w2": f32(P['rwkv_w2']),
              "rwkv_a2": f32(P['rwkv_a2']), "rwkv_g2": f32(P['rwkv_g2']), "w_branch": f32(P['w_branch']), "w_out": f32(P['w_out']),
              "w_ffn_in": f32(P['w_ffn_in']), "w_ffn_out": f32(P['w_ffn_out'])}
    in_maps = []
    for core in range(NCORES):
        m = dict(shared)
        cc = np.zeros((128, NCC), np.float32)
        if core < 4:
            b = core
            x = P['x_sample'][b]
            cond = P['c'][b]
            cc[:, CC_KEEP] = 1.0
            cc[:, CC_MUF:CC_MUF + 4] = 1.0
            m["hs0"] = f32(P['state_hgrn'][b])
            m["rs0"] = f32(np.swapaxes(P['state_rwkv'][b], -1, -2))
            m["cs0"] = f32(P['state_mlstm_C'][b])
            m["ns0"] = f32(P['state_mlstm_n'][b][..., None])
            cc[:, CC_M0:CC_M0 + 16] = P['state_mlstm_m'][b].reshape(1, 16)
        else:
            g = core - 4
            x = P['x_prompt'][g * 8:(g + 1) * 8].reshape(TOK, D)
            cond = P['c_ctx']
            cc[:, CC_MUF:CC_MUF + 2] = 1.0
            m["hs0"] = np.zeros((DEPTH, 2, 4, 128, 128), np.float32)
            m["rs0"] = np.zeros((DEPTH, 2, 8, 64, 64), np.float32)
            m["cs0"] = np.zeros((DEPTH, 2, 4, 128, 128), np.float32)
            m["ns0"] = np.zeros((DEPTH, 2, 4, 128, 1), np.float32)
            for r in range(1, 32):
                cc[:, CC_BML + r - 1] = 1.0 if (r * 64) % 256 != 0 else 0.0
            for r in range(0, 31):
                cc[:, CC_BMR + r] = 1.0 if ((r + 1) * 64) % 256 != 0 else 0.0
        m["xT"] = f32(x.T)
        cc[:, CC_COND:CC_COND + 8] = _cols(cond, 8)
        m["cc"] = cc
        in_maps.append(m)
    res = run_bass_kernel_spmd(nc, in_maps, core_ids=list(range(NCORES)))
    R = res.results
    kernel.last_results = R
    y_sample = np.stack([R[c]["yT"].T for c in range(4)], 0)
    y_prompt = np.concatenate([R[c]["yT"].T.reshape(8, 256, D) for c in range(4, 8)], 0)

    def gather(name, tail):
        outs = []
        for c in range(4, 8):
            a = R[c][name]
            outs.append(np.moveaxis(a, 2, 0))
        return np.concatenate(outs, 0)
    new_h = gather("so_h", None)
    new_r = np.swapaxes(gather("so_r", None), -1, -2)
    new_C = gather("so_c", None)
    new_n = gather("so_n", None)[..., 0]
    mm_ = []
    for c in range(4, 8):
        a = R[c]["so_m"]
        mm_.append(np.transpose(a[:, 8:16, :].reshape(DEPTH, 2, 4, NSEG), (3, 0, 1, 2)))
    new_m = np.concatenate(mm_, 0)
    return (f32(y_prompt), f32(y_sample), f32(new_h), f32(new_r), f32(new_C), f32(new_n), f32(new_m))
```

```python
import numpy as np
import concourse.bass as bass
import concourse.mybir as mybir
from concourse.bass_utils import run_bass_kernel_spmd
from contextlib import ExitStack

F32 = mybir.dt.float32
BF16 = mybir.dt.bfloat16
AF = mybir.ActivationFunctionType
ALU = mybir.AluOpType
AX = mybir.AxisListType

NCORES = 8
TOK = 2048
NTB = 4
TB = 512
NTT = 16
D = 1024
KC = 8
DEPTH = 2
D_IN = 9616
D_FF = 2816
NFF = 22
C_HQ, C_HFF, C_HFB, C_HI, C_HG = 0, 512, 1024, 1536, 2048
C_RW = 2560
C_MQ, C_MK, C_MV, C_MO, C_MG = 4480, 4992, 5504, 6016, 6528
C_GATES = 6544
KAPPA = float(np.exp(-0.5))
NSEG = 8

PC_NW, PC_BADA, PC_LB0, PC_LB1, PC_HNORM, PC_MU, PC_W0, PC_A0, PC_KK, PC_KA, PC_RK, PC_LN0, PC_LN1, PC_MNORM, PC_GBF, PC_GBI = \
    0, 32, 80, 84, 88, 92, 152, 160, 168, 172, 176, 180, 184, 188, 192, 193
NPC = 194
CC_COND, CC_KEEP, CC_MUF, CC_M0, CC_BML, CC_BMR = 0, 8, 9, 13, 29, 60
NCC = 91
CS_ID, CS_ONES, CS_BONES, CS_M64F, CS_M64B, CS_M128F, CS_M128B, CS_RWF, CS_RWB, CS_SELF, CS_SELI, CS_SGN, CS_MDF, CS_MDB = \
    0, 128, 256, 384, 448, 512, 640, 768, 1024, 1280, 1296, 1312, 1313, 1314
NCST = 1315

DBG = {}


class T:
    __slots__ = ('t', 'writer', 'readers')

    def __init__(self, t):
        self.t = t
        self.writer = None
        self.readers = {}

    def __getitem__(self, idx):
        return self.t[idx]


class KB:
    def __init__(self, n_dma_sems=48):
        self.nc = bass.Bass("TRN2", target_bir_lowering=False)
        nc = self.nc
        self.eng = {'pe': nc.tensor, 'act': nc.scalar, 'dve': nc.vector, 'pool': nc.gpsimd, 'sp': nc.sync}
        self.sem = {}
        self.cnt = {}
        for e in ['pe', 'act', 'dve', 'pool']:
            self.sem[e] = nc.alloc_semaphore(name='s_' + e)
            self.cnt[e] = 0
        self.ndma = n_dma_sems
        for i in range(n_dma_sems):
            k = 'd%d' % i
            self.sem[k] = nc.alloc_semaphore(name='s_' + k)
            self.cnt[k] = 0
        self.dma_rr = 0
        self.dma_rr_sw = 0
        self.seen = {e: {} for e in self.eng}
        self.pending = None
        self.snap = {}
        self.nins = 0
        self.out_toks = []
        self.banks = [nc.alloc_psum_tensor('psb%d' % i, [128, 512], F32) for i in range(8)]
        self.pst = [T(None) for _ in range(8)]
        self.ps_rr = 0
        self.psb_rr = 0

    def sb(self, name, shape, dt=F32, stack=None):
        self.nalloc = getattr(self, 'nalloc', 0) + 1
        name = "%s_u%d" % (name, self.nalloc)
        if stack is None:
            return T(self.nc.alloc_sbuf_tensor(name, list(shape), dt))
        return T(stack.enter_context(self.nc.sbuf_tensor(name, list(shape), dt)))

    def ps_small(self):
        i = self.ps_rr
        self.ps_rr = (i + 1) % 16
        b = i % 8
        h = i // 8
        ap = self.banks[b][:, h * 256:h * 256 + 256]
        return [self.pst[b]], ap

    def ps_big(self):
        b = self.psb_rr
        self.psb_rr = (b + 1) % 8
        return [self.pst[b]], self.banks[b][:, :]

    def _wait(self, e, key, val, raw=False):
        if key == e and e == 'pe':
            return
        if self.seen[e].get(key, 0) >= val:
            return
        se = self.seen[e]
        se[key] = val
        sn = self.snap.get((key, val))
        if sn:
            for k2, v2 in sn.items():
                if k2 != e and se.get(k2, 0) < v2:
                    se[k2] = v2
        if self.pending is not None:
            self.pending.append((key, val))
        else:
            self.eng[e].wait_ge(self.sem[key], val)

    def _deps(self, e, reads, writes):
        for r in reads:
            if r.writer is not None:
                self._wait(e, r.writer[0], r.writer[1], raw=True)
        for w in writes:
            if w.writer is not None:
                self._wait(e, w.writer[0], w.writer[1])
            for k, v in w.readers.items():
                self._wait(e, k, v)

    def op(self, e, fn, reads=(), writes=(), inc=True):
        fuse = e in ('act', 'dve', 'pool', 'pe')
        self.pending = [] if fuse else None
        self._deps(e, reads, writes)
        pend = self.pending or []
        self.pending = None
        best = {}
        for k, v in pend:
            best[k] = max(best.get(k, 0), v)
        pend = list(best.items())
        for k, v in pend[:-1]:
            self.eng[e].wait_ge(self.sem[k], v)
        ins = fn(self.eng[e])
        if pend:
            k, v = pend[-1]
            ins.wait_op(self.sem[k], v, "sem-ge")
        if inc:
            self.cnt[e] += 1
            ins.then_inc(self.sem[e], 1)
            c = self.cnt[e]
            self.snap[(e, c)] = dict(self.seen[e])
        else:
            c = self.cnt[e] + 1
        for r in reads:
            r.readers[e] = c
        for w in writes:
            w.writer = (e, c)
            w.readers = {}
        self.nins += 1

    def dma(self, q, out, in_, reads=(), writes=()):
        self._deps(q, reads, writes)
        half = self.ndma // 2
        if q == 'pool':
            k = 'd%d' % (half + self.dma_rr_sw)
            self.dma_rr_sw = (self.dma_rr_sw + 1) % half
        else:
            k = 'd%d' % self.dma_rr
            self.dma_rr = (self.dma_rr + 1) % half
        if self.cnt[k] > 0:
            self._wait(q, k, self.cnt[k])
        ins = self.eng[q].dma_start(out=out, in_=in_)
        self.cnt[k] += 16
        ins.then_inc(self.sem[k], 16)
        tok = (k, self.cnt[k])
        self.snap[tok] = dict(self.seen[q])
        for r in reads:
            r.readers[k] = self.cnt[k]
        for w in writes:
            w.writer = tok
            w.readers = {}
        self.nins += 1
        return tok

    def barrier(self):
        for e in self.eng:
            for k, v in self.cnt.items():
                if v > 0:
                    self._wait(e, k, v, raw=True)

    def mm(self, out, lhsT, rhs, start, stop, R, W, inc=None):
        if inc is None:
            inc = True if DBG.get("allinc", 0) else bool(stop)
        self.op('pe', lambda e: e.matmul(out, lhsT=lhsT, rhs=rhs, start=start, stop=stop), R, W, inc=inc)

    def tr(self, out, in_, ident, R, W):
        self.op('pe', lambda e: e.transpose(out, in_, ident), R, W)

    def act(self, out, in_, func, R, W, bias=None, scale=None, eng='act'):
        kw = {}
        if bias is not None:
            kw['bias'] = bias
        if scale is not None:
            kw['scale'] = scale
        self.op(eng, lambda e: e.activation(out=out, in_=in_, func=func, **kw), R, W)

    def tt(self, out, in0, in1, op, R, W, eng='dve'):
        self.op(eng, lambda e: e.tensor_tensor(out=out, in0=in0, in1=in1, op=op), R, W)

    def ts(self, out, in0, s1, s2, op0, op1, R, W, eng='dve'):
        if op1 is None:
            self.op(eng, lambda e: e.tensor_scalar(out=out, in0=in0, scalar1=s1, scalar2=None, op0=op0), R, W)
        else:
            self.op(eng, lambda e: e.tensor_scalar(out=out, in0=in0, scalar1=s1, scalar2=s2, op0=op0, op1=op1), R, W)

    def stt(self, out, in0, scalar, in1, op0, op1, R, W):
        self.op('dve', lambda e: e.scalar_tensor_tensor(out=out, in0=in0, scalar=scalar, in1=in1, op0=op0, op1=op1), R, W)

    def cp(self, out, in_, R, W, eng='dve'):
        if eng == 'act':
            self.act(out, in_, AF.Copy, R, W)
        else:
            self.op(eng, lambda e: e.tensor_copy(out=out, in_=in_), R, W)

    def memset(self, ap, val, W, eng='pool'):
        self.op(eng, lambda e: e.memset(ap, val), [], W)


class Prog:
    def __init__(self):
        self.kb = KB()
        self.nc = self.kb.nc
        self.taps = {}

    def din(self, name, shape):
        return self.nc.dram_tensor(name, list(shape), F32, kind="ExternalInput").ap()

    def dout(self, name, shape):
        return self.nc.dram_tensor(name, list(shape), F32, kind="ExternalOutput").ap()

    def tap(self, name, t, ap, shape):
        if not DBG.get('taps'):
            return
        o = self.dout('tap_' + name, shape)
        self.kb.out_toks.append(self.kb.dma('pool', o, ap, reads=[t]))

    def wl(self, segs):
        t = self.wring[self.wr_i]
        self.wr_i = (self.wr_i + 1) % len(self.wring)
        for dst_fn, src in segs:
            self.kb.dma('pool', dst_fn(t.t), src, writes=[t])
        return t

    def w_in_cols(self, l, cols, width=128):
        src = self.w_in[l].rearrange("(kc p) c -> p kc c", p=128)
        segs = []
        if DBG.get('contig') and len(cols) > 1 and all(cols[j + 1] == cols[j] + width for j in range(len(cols) - 1)):
            n = len(cols) * width
            return self.wl([(lambda tl: tl[:, :].rearrange("p (k c) -> p k c", c=512)[:, :, 0:n], src[:, :, cols[0]:cols[0] + n])])
        for j, c0 in enumerate(cols):
            segs.append((lambda tl, j=j: tl[:, :].rearrange("p (k c) -> p k c", c=512)[:, :, j * width:(j + 1) * width],
                         src[:, :, c0:c0 + width]))
        return self.wl(segs)

    def wv(self, t):
        return t.t[:, :].rearrange("p (k c) -> p k c", c=512)

    def proj(self, wt, j, tb, width=128, M=128):
        kb = self.kb
        pt, ps = kb.ps_big()
        wvw = self.wv(wt)
        h = self.hT[tb]
        for kc in range(KC):
            kb.mm(ps[0:M, :], wvw[:, kc, j * width:j * width + M], h.t[:, kc, :], kc == 0, kc == KC - 1, [wt, h], pt)
        return pt, ps[0:M, :]

    def build(self):
        kb, nc = self.kb, self.nc
        self.xT_in = self.din("xT", [D, TOK])
        self.pc_in = self.din("pc", [DEPTH, 128, NPC])
        self.cc_in = self.din("cc", [128, NCC])
        self.cst_in = self.din("cst", [128, NCST])
        self.w_ada = self.din("w_ada", [DEPTH, D, 6 * D])
        self.w_in = self.din("w_in", [DEPTH, D, D_IN])
        self.w2 = self.din("rwkv_w2", [DEPTH, 2, 64, 512])
        self.a2 = self.din("rwkv_a2", [DEPTH, 2, 64, 512])
        self.g2 = self.din("rwkv_g2", [DEPTH, 128, 512])
        self.w_branch = self.din("w_branch", [DEPTH, 3, 512, D])
        self.w_out = self.din("w_out", [DEPTH, D, D])
        self.w_ffn_in = self.din("w_ffn_in", [DEPTH, D, 2 * D_FF])
        self.w_ffn_out = self.din("w_ffn_out", [DEPTH, D_FF, D])
        self.hs0 = self.din("hs0", [DEPTH, 2, 4, 128, 128])
        self.rs0 = self.din("rs0", [DEPTH, 2, 8, 64, 64])
        self.cs0 = self.din("cs0", [DEPTH, 2, 4, 128, 128])
        self.ns0 = self.din("ns0", [DEPTH, 2, 4, 128, 1])
        self.yT_out = self.dout("yT", [D, TOK])
        self.so_h = self.dout("so_h", [DEPTH, 2, NSEG, 4, 128, 128])
        self.so_r = self.dout("so_r", [DEPTH, 2, NSEG, 8, 64, 64])
        self.so_c = self.dout("so_c", [DEPTH, 2, NSEG, 4, 128, 128])
        self.so_n = self.dout("so_n", [DEPTH, 2, NSEG, 4, 128, 1])
        self.so_m = self.dout("so_m", [DEPTH, 16, NSEG])
        self.xs = nc.dram_tensor("xspill", [NTB, 128, KC, TB], F32, kind="Internal").ap()

        self.cst = kb.sb("cst_sb", [128, NCST])
        self.cstb = kb.sb("cstb", [128, NCST], BF16)
        self.pcl = [kb.sb("pcl%d" % l, [128, NPC]) for l in range(DEPTH)]
        self.cc = kb.sb("cc_sb", [128, NCC])
        kb.dma('sp', self.cst[:], self.cst_in, writes=[self.cst])
        kb.dma('pool', self.cstb[:], self.cst_in, writes=[self.cstb])
        for l in range(DEPTH):
            kb.dma('sp', self.pcl[l][:], self.pc_in[l], writes=[self.pcl[l]])
        kb.dma('sp', self.cc[:], self.cc_in, writes=[self.cc])
        self.wring = [kb.sb("wring%d" % i, [128, 4096], BF16) for i in range(3)]
        self.wr_i = 0
        self.hT = [kb.sb("hT%d" % tb, [128, KC, TB], BF16) for tb in range(NTB)]
        self.regM = kb.sb("regM", [128, 4 * KC * TB], BF16)
        self.merged = [T(self.regM.t[:, tb * KC * TB:(tb + 1) * KC * TB].rearrange("p (k t) -> p k t", t=TB)) for tb in range(NTB)]
        self.modT = [kb.sb("modT%d" % l, [128, 48]) for l in range(DEPTH)]
        self.cols = [kb.sb("cols%d" % l, [128, 64]) for l in range(DEPTH)]
        self.mue = [kb.sb("mue%d" % l, [128, 5 * 15]) for l in range(DEPTH)]
        self.scond = kb.sb("scond", [128, KC], BF16)
        self.rstd = [kb.sb("rstd%d" % i, [128, TB]) for i in range(2)]
        self.rs_i = 0
        self.tmpA = [kb.sb("tmpA%d" % i, [128, TB]) for i in range(2)]
        self.ta_i = 0

        self.ident_b = self.cstb.t[:, CS_ID:CS_ID + 128]
        self.ones_b = self.cstb.t[:, CS_ONES:CS_ONES + 128]
        self.bones_b = self.cstb.t[:, CS_BONES:CS_BONES + 128]

        kb.act(self.scond[:], self.cc.t[:, CC_COND:CC_COND + 8], AF.Silu, [self.cc], [self.scond])

        with ExitStack() as st0:
            self.xT = [kb.sb("xT%d" % tb, [128, KC, TB], F32, st0) for tb in range(NTB)]
            xin = self.xT_in.rearrange("(kc p) t -> p kc t", p=128)
            for tb in range(NTB):
                kb.dma('sp', self.xT[tb][:], xin[:, :, tb * TB:(tb + 1) * TB], writes=[self.xT[tb]])
            self.compute_mod(0)
            self.layer_cols(0)
            self.norm_to_h(0, 0)
            self.spill_x()
            kb.barrier()
        for l in range(DEPTH):
            self.mixer(l)
            with ExitStack() as st1:
                self.xT = [kb.sb("xT%d_%d" % (tb, l), [128, KC, TB], F32, st1) for tb in range(NTB)]
                self.reload_x()
                if l + 1 < DEPTH:
                    self.compute_mod(l + 1)
                    self.layer_cols(l + 1)
                self.out_proj(l)
                self.ffn(l, st1)
                if l + 1 < DEPTH:
                    self.norm_to_h(l + 1, 0)
                    self.spill_x()
                else:
                    yo = self.yT_out.rearrange("(kc p) t -> p kc t", p=128)
                    for tb in range(NTB):
                        kb.out_toks.append(kb.dma('sp', yo[:, :, tb * TB:(tb + 1) * TB], self.xT[tb][:], reads=[self.xT[tb]]))
                kb.barrier()
        for k, v in kb.out_toks:
            kb._wait('sp', k, v)
        return nc

    def compute_mod(self, l):
        kb = self.kb
        pt, ps = kb.ps_small()
        src = self.w_ada[l].rearrange("(kc p) c -> p kc c", p=128)
        for blk in range(12):
            wt = self.wl([(lambda tl: tl[:, :].rearrange("p (k c) -> p k c", c=512), src[:, :, blk * 512:(blk + 1) * 512])])
            wv = self.wv(wt)
            for jj in range(4):
                j = blk * 4 + jj
                for kc in range(KC):
                    kb.mm(ps[:, j:j + 1], wv[:, kc, jj * 128:(jj + 1) * 128], self.scond.t[:, kc:kc + 1], kc == 0, kc == KC - 1,
                          [wt, self.scond], pt)
        kb.tt(self.modT[l][:], ps[:, 0:48], self.pcl[l].t[:, PC_BADA:PC_BADA + 48], ALU.add, pt + [self.pcl[l]], [self.modT[l]])

    def layer_cols(self, l):
        kb = self.kb
        m, c, p = self.modT[l], self.cols[l], self.pcl[l]
        kb.stt(c.t[:, 0:8], m.t[:, 8:16], 1.0, p.t[:, PC_NW:PC_NW + 8], ALU.add, ALU.mult, [m, p], [c])
        kb.tt(c.t[:, 8:16], m.t[:, 16:24], p.t[:, PC_NW + 8:PC_NW + 16], ALU.mult, [m, p], [c])
        kb.stt(c.t[:, 16:24], m.t[:, 32:40], 1.0, p.t[:, PC_NW + 16:PC_NW + 24], ALU.add, ALU.mult, [m, p], [c])
        kb.tt(c.t[:, 24:32], m.t[:, 40:48], p.t[:, PC_NW + 24:PC_NW + 32], ALU.mult, [m, p], [c])
        kb.tt(c.t[:, 32:36], p.t[:, PC_LB1:PC_LB1 + 4], p.t[:, PC_LB0:PC_LB0 + 4], ALU.subtract, [p], [c])
        kb.act(c.t[:, 32:36], c.t[:, 32:36], AF.Sigmoid, [c], [c])
        if l == 0:
            kb.ts(c.t[:, 32:36], c.t[:, 32:36], 0.0, None, ALU.mult, None, [c], [c])
        kb.ts(c.t[:, 36:40], c.t[:, 32:36], -1.0, 1.0, ALU.mult, ALU.add, [c], [c])
        me = self.mue[l]
        for j in range(4):
            kb.ts(me.t[:, j * 15:(j + 1) * 15], p.t[:, PC_MU + j * 15:PC_MU + (j + 1) * 15], self.cc.t[:, CC_MUF + j:CC_MUF + j + 1], None,
                  ALU.mult, None, [p, self.cc], [me])
        kb.tt(me.t[:, 60:75], me.t[:, 0:15], me.t[:, 15:30], ALU.add, [me], [me])
        kb.tt(me.t[:, 60:75], me.t[:, 60:75], me.t[:, 30:45], ALU.add, [me], [me])
        kb.tt(me.t[:, 60:75], me.t[:, 60:75], me.t[:, 45:60], ALU.add, [me], [me])
        kb.ts(me.t[:, 60:75], me.t[:, 60:75], -1.0, 1.0, ALU.mult, ALU.add, [me], [me])

    def rstd_of(self, src_t, src_ap, nk, scale, eps, sq, sq_ap, lhs=None):
        kb = self.kb
        kb.act(sq_ap, src_ap, AF.Square, [src_t], [sq])
        pt, ps = kb.ps_big()
        for kc in range(nk):
            kb.mm(ps, self.ones_b if lhs is None else lhs, sq_ap[:, kc, :], kc == 0, kc == nk - 1, [self.cstb, sq], pt)
        r = self.rstd[self.rs_i]
        self.rs_i ^= 1
        kb.act(r[:], ps, AF.Sqrt, pt, [r], bias=eps, scale=scale)
        kb.op('dve', lambda e: e.reciprocal(out=r[:], in_=r[:]), [r], [r])
        return r

    def norm_to_h(self, l, which):
        kb = self.kb
        c, m = self.cols[l], self.modT[l]
        a0 = 0 if which == 0 else 16
        s0 = 0 if which == 0 else 24
        for tb in range(NTB):
            x = self.xT[tb]
            r = self.rstd_of(x, x[:], KC, 1.0 / D, 1e-6, self.hT[tb], self.hT[tb][:])
            for kc in range(KC):
                tmp = self.tmpA[self.ta_i]
                self.ta_i ^= 1
                kb.stt(tmp[:], x.t[:, kc, :], c.t[:, a0 + kc:a0 + kc + 1], r[:], ALU.mult, ALU.mult, [x, c, r], [tmp])
                kb.act(self.hT[tb].t[:, kc, :], tmp[:], AF.Identity, [tmp, m], [self.hT[tb]], bias=m.t[:, s0 + kc:s0 + kc + 1])

    def spill_x(self):
        for tb in range(NTB):
            self.kb.dma('sp', self.xs[tb], self.xT[tb][:], reads=[self.xT[tb]])

    def reload_x(self):
        self.kb.barrier()
        for tb in range(NTB):
            self.kb.dma('sp', self.xT[tb][:], self.xs[tb], writes=[self.xT[tb]])

    def resid_update(self, l, tb, yt, gcol0):
        kb = self.kb
        c = self.cols[l]
        r = self.rstd_of(yt, yt[:], KC, 1.0 / D, 1e-6, self.hT[tb], self.hT[tb][:])
        x = self.xT[tb]
        for kc in range(KC):
            tmp = self.tmpA[self.ta_i]
            self.ta_i ^= 1
            kb.stt(tmp[:], yt.t[:, kc, :], c.t[:, gcol0 + kc:gcol0 + kc + 1], r[:], ALU.mult, ALU.mult, [yt, c, r], [tmp])
            kb.tt(x.t[:, kc, :], x.t[:, kc, :], tmp[:], ALU.add, [x, tmp], [x], eng='pool')

    def out_proj(self, l):
        kb = self.kb
        with ExitStack() as st:
            ytmp = kb.sb("ytmp_o%d" % l, [128, KC, TB], F32, st)
            src = self.w_out[l].rearrange("(kc p) c -> p kc c", p=128)
            wts = [self.wl([(lambda tl: tl[:, :].rearrange("p (k c) -> p k c", c=512), src[:, :, half * 512:(half + 1) * 512])])
                   for half in range(2)]
            for tb in range(NTB):
                mg = self.merged[tb]
                for half in range(2):
                    wt = wts[half]
                    wv = self.wv(wt)
                    for mm_ in range(4):
                        m = half * 4 + mm_
                        pt, ps = kb.ps_big()
                        for kc in range(KC):
                            kb.mm(ps, wv[:, kc, mm_ * 128:(mm_ + 1) * 128], mg.t[:, kc, :], kc == 0, kc == KC - 1, [wt, mg], pt)
                        kb.cp(ytmp.t[:, m, :], ps, pt, [ytmp], eng='act')
                self.resid_update(l, tb, ytmp, 8)
            kb.barrier()

    def ffn(self, l, st1):
        kb = self.kb
        self.norm_to_h(l, 1)
        with ExitStack() as st:
            actB = kb.sb("ffn_actB%d" % l, [128, NFF - 8, 2 * TB], BF16, st)
            sgt = [kb.sb("ffn_sg%d_%d" % (l, i), [128, TB], F32, st) for i in range(2)]
            kb.barrier()
            ytmp = T(self.regM.t[:, 0:2 * KC * TB].bitcast(F32).rearrange("p (k t) -> p k t", t=TB))
            actA = T(self.regM.t[:, 2 * KC * TB:4 * KC * TB].rearrange("p (j t) -> p j t", t=2 * TB))

            def act_of(j):
                return (actA, actA.t[:, j, :]) if j < 8 else (actB, actB.t[:, j - 8, :])
            wo_src = self.w_ffn_out[l].rearrange("(j p) c -> p j c", p=128)
            for half in range(2):
                for jg in range(NFF // 2):
                    j0 = jg * 2
                    wt = self.w_cols(self.w_ffn_in[l], [j0 * 128, (j0 + 1) * 128, D_FF + j0 * 128, D_FF + (j0 + 1) * 128])
                    for jj in range(2):
                        j = j0 + jj
                        at_, aap = act_of(j)
                        for tbl in range(2):
                            tb = half * 2 + tbl
                            ptu, pu = self.proj(wt, jj, tb)
                            ptg, pg = self.proj(wt, 2 + jj, tb)
                            sg = sgt[tbl]
                            kb.act(sg[:], pg, AF.Silu, ptg, [sg])
                            kb.tt(aap[:, tbl * TB:(tbl + 1) * TB], pu, sg[:], ALU.mult, ptu + [sg], [at_])
                for tbl in range(2):
                    tb = half * 2 + tbl
                    accs = [kb.ps_big() for _ in range(8)]
                    for jg in range(6):
                        nj = 4 if jg < 5 else 2
                        wt = self.wl([(lambda tl, nj=nj: tl[:, 0:nj * 1024].rearrange("p (j c) -> p j c", c=1024), wo_src[:, jg * 4:jg * 4 + nj, :])])
                        wv = wt.t[:, :].rearrange("p (j c) -> p j c", c=1024)
                        for jj in range(nj):
                            j = jg * 4 + jj
                            at_, aap = act_of(j)
                            for m in range(8):
                                kb.mm(accs[m][1], wv[:, jj, m * 128:(m + 1) * 128], aap[:, tbl * TB:(tbl + 1) * TB], j == 0, j == NFF - 1,
                                      [wt, at_], accs[m][0], inc=(m == 7 or j == NFF - 1))
                    for m in range(8):
                        kb.cp(ytmp.t[:, m, :], accs[m][1], accs[m][0], [ytmp], eng='act' if m % 2 else 'dve')
                    self.resid_update(l, tb, ytmp, 24)
            kb.barrier()

    def w_cols(self, wsrc, cols, width=128):
        src = wsrc.rearrange("(kc p) c -> p kc c", p=128)
        segs = []
        if DBG.get('contig') and len(cols) > 1 and all(cols[j + 1] == cols[j] + width for j in range(len(cols) - 1)):
            n = len(cols) * width
            return self.wl([(lambda tl: tl[:, :].rearrange("p (k c) -> p k c", c=512)[:, :, 0:n], src[:, :, cols[0]:cols[0] + n])])
        for j, c0 in enumerate(cols):
            segs.append((lambda tl, j=j: tl[:, :].rearrange("p (k c) -> p k c", c=512)[:, :, j * width:(j + 1) * width],
                         src[:, :, c0:c0 + width]))
        return self.wl(segs)

    def mixer(self, l):
        kb = self.kb
        with ExitStack() as st:
            self.rmF = kb.sb("rmF%d" % l, [128, TB], BF16, st)
            self.rmB = kb.sb("rmB%d" % l, [128, TB], BF16, st)
            self.ycur = [kb.sb("ycur%d_%d" % (l, c), [128, TOK], BF16, st) for c in range(4)]
            self.wbt = kb.sb("wbt%d" % l, [128, 4096], BF16, st)
            if DBG.get('stub_mixer'):
                for tb in range(NTB):
                    kb.memset(self.merged[tb][:], 0.0, [self.merged[tb]])
                kb.barrier()
                return
            nbr = DBG.get('branches', 'ABC')
            first = True
            for i, nm in enumerate('ABC'):
                if nm not in nbr:
                    continue
                with ExitStack() as sb_:
                    if nm == 'A':
                        self.branch_hgrn(l, sb_)
                    elif nm == 'B':
                        self.branch_rwkv(l, sb_)
                    else:
                        self.branch_mlstm(l, sb_)
                    kb.barrier()
                for c in range(4):
                    self.tap("y%s_l%d_c%d" % (nm, l, c), self.ycur[c], self.ycur[c][:], [128, TOK])
                self.merge(l, i, first)
                first = False
            kb.barrier()

    def make_rmask(self, L):
        kb = self.kb
        kb.memset(self.rmF[:], 1.0, [self.rmF])
        kb.memset(self.rmB[:], 1.0, [self.rmB])
        vF = self.rmF.t[:, :].rearrange("p (a b) -> p a b", b=L)
        vB = self.rmB.t[:, :].rearrange("p (a b) -> p a b", b=L)
        kb.memset(vF[:, :, 0:1], 0.0, [self.rmF])
        kb.memset(vB[:, :, L - 1:L], 0.0, [self.rmB])

    def cumsum_chunks(self, out_t, in_t, d, np_=128):
        kb = self.kb
        for tb in range(NTB):
            sl = slice(tb * TB, (tb + 1) * TB)
            if d == 0:
                kb.op('dve', lambda e: e.tensor_tensor_scan(out=out_t.t[0:np_, sl], data0=self.rmF.t[0:np_, :], data1=in_t.t[0:np_, sl],
                                                            initial=0.0, op0=ALU.mult, op1=ALU.add), [self.rmF, in_t], [out_t])
            else:
                kb.op('dve', lambda e: e.tensor_tensor_scan(out=out_t.t[0:np_, sl][:, ::-1], data0=self.rmB.t[0:np_, ::-1],
                                                            data1=in_t.t[0:np_, sl][:, ::-1],
                                                            initial=0.0, op0=ALU.mult, op1=ALU.add), [self.rmB, in_t], [out_t])

    def to_tm(self, dst_t, dst_ap, src_t, src_ap, eng='act'):
        kb = self.kb
        pt, ps = kb.ps_small()
        psb = ps.bitcast(BF16)[:, 0:128]
        kb.tr(psb, src_ap, self.ident_b, [src_t, self.cstb], pt)
        kb.cp(dst_ap, psb, pt, [dst_t], eng=eng)

    def head_post(self, l, oacc, normcol, gate_t, ydst, sq):
        kb = self.kb
        for tb in range(NTB):
            o = oacc[tb]
            r = self.rstd_of(o, o.t[:, :].rearrange("p (k t) -> p k t", k=1), 1, 1.0 / 128, 1e-6, sq, sq.t[:, :].rearrange("p (k t) -> p k t", k=1))
            tmp = self.tmpA[self.ta_i]
            self.ta_i ^= 1
            kb.stt(tmp[:], o[:], normcol, r[:], ALU.mult, ALU.mult, [o, self.pcl[l], r], [tmp])
            kb.tt(ydst.t[:, tb * TB:(tb + 1) * TB], tmp[:], gate_t.t[:, tb * TB:(tb + 1) * TB], ALU.mult, [tmp, gate_t], [ydst])

    def gla_run(self, G, st):
        kb = self.kb
        NV = G['NV']
        den = G.get('den')
        Sall = [kb.sb("g_Sall%d" % d, [128, 16, NV], BF16, st) for d in range(2)]
        if den:
            drow = [kb.sb("g_drow%d" % d, [1, 128], F32, st) for d in range(2)]
            rrep = [kb.sb("g_rrep%d" % d, [128, 128], F32, st) for d in range(2)]
            numr = [kb.sb("g_numr%d" % d, [128, 128], F32, st) for d in range(2)]
        ones_row = self.cst.t[0:1, CS_ONES:CS_ONES + 128]
        ones_colb = self.cstb.t[:, CS_ONES:CS_ONES + 1]

        def passA(d, tile, bi):
            KT, QT, Vtm, Ktm, ATs = G['KT'][d], G['QT'][d], G['Vtm'], G['Ktm'][d][bi], G['ATs'][d][bi]
            S32, e1 = G['S32'][d], G['e1'][d]
            tsl = slice(tile * 128, (tile + 1) * 128)
            KTe = G['KTe'][d]
            self.to_tm(Ktm, Ktm[:], KTe, KTe.t[:, tsl])
            pt, ps = kb.ps_small()
            kb.mm(ps[:, 0:128], KT.t[:, tsl], QT.t[:, tsl], True, True, [KT, QT], pt)
            kb.tt(ATs[:], ps[:, 0:128], G['mask'][d], ALU.mult, pt + [self.cst], [ATs])
            order = range(4) if d == 0 else range(3, -1, -1)
            for q in order:
                ch = tile * 4 + q
                s_cur = S32[G['cur'][d]]
                s_nxt = S32[1 - G['cur'][d]]
                pt2, ps2 = kb.ps_small()
                rs_ = slice(32 * q, 32 * q + 32)
                kb.op('pe', lambda e: e.matmul(ps2[:, 0:NV], lhsT=Ktm.t[rs_, :], rhs=Vtm.t[rs_, tile, 0:NV], start=True, stop=True,
                                               tile_position=(32 * q, 0)), [Ktm, Vtm], pt2)
                kb.cp(Sall[d].t[:, ch % 16, :], s_cur[:], [s_cur], [Sall[d]], eng='act')
                ec = e1.t[:, ch:ch + 1]
                kb.stt(s_nxt[:], s_cur[:], ec, ps2[:, 0:NV], ALU.mult, ALU.add, pt2 + [e1, s_cur], [s_nxt])
                G['cur'][d] = 1 - G['cur'][d]
                seg_end = (d == 0 and tile % 2 == 1 and q == 3) or (d == 1 and tile % 2 == 0 and q == 0)
                if seg_end:
                    G['seg_out'](d, tile // 2, s_nxt)
                    nx2 = S32[1 - G['cur'][d]]
                    kb.ts(nx2[:], s_nxt[:], self.cc.t[:, CC_KEEP:CC_KEEP + 1], None, ALU.mult, None, [s_nxt, self.cc], [nx2])
                    G['cur'][d] = 1 - G['cur'][d]

        def passB(d, tile, bi):
            QT, Vtm, ATs = G['QT'][d], G['Vtm'], G['ATs'][d][bi]
            pto, pso = kb.ps_small()
            kb.mm(pso[:, 0:128], Vtm.t[:, tile, 0:128], ATs[:], True, False, [Vtm, ATs], pto)
            for q in range(4):
                ch = tile * 4 + q
                kb.mm(pso[:, q * 32:q * 32 + 32], Sall[d].t[:, ch % 16, 0:128], QT.t[:, ch * 32:ch * 32 + 32], False, q == 3, [Sall[d], QT], pto)
            ob = G['oacc'][tile // 4]
            osl = slice((tile % 4) * 128, (tile % 4) * 128 + 128)
            if not den:
                kb.tt(ob.t[:, osl], ob.t[:, osl], pso[:, 0:128], ALU.add, [ob] + pto, [ob])
                return
            ptd, psd = kb.ps_small()
            kb.mm(psd[0:1, 0:128], ones_colb, ATs[:], True, False, [self.cstb, ATs], ptd)
            for q in range(4):
                ch = tile * 4 + q
                kb.mm(psd[0:1, q * 32:q * 32 + 32], Sall[d].t[:, ch % 16, 128:129], QT.t[:, ch * 32:ch * 32 + 32], False, q == 3, [Sall[d], QT], ptd)
            dr = drow[d]
            kb.act(dr[:], psd[0:1, 0:128], AF.Abs, ptd, [dr])
            kb.ts(dr[:], dr[:], 1.0, None, ALU.max, None, [dr], [dr])
            kb.op('dve', lambda e: e.reciprocal(out=dr[:], in_=dr[:]), [dr], [dr])
            ptr, psr = kb.ps_small()
            kb.mm(psr[:, 0:128], ones_row, dr[:], True, True, [self.cst, dr], ptr)
            kb.cp(rrep[d][:], psr[:, 0:128], ptr, [rrep[d]], eng='act')
            kb.tt(numr[d][:], pso[:, 0:128], rrep[d][:], ALU.mult, pto + [rrep[d]], [numr[d]])
            kb.tt(ob.t[:, osl], ob.t[:, osl], numr[d][:], ALU.add, [ob, numr[d]], [ob], eng='pool')

        tiles = [list(range(NTT)), list(range(NTT - 1, -1, -1))]
        for step in range(NTT + 1):
            for d in range(2):
                if step < NTT:
                    passA(d, tiles[d][step], step % 2)
                if step >= 1:
                    passB(d, tiles[d][step - 1], (step - 1) % 2)

    def branch_hgrn(self, l, st):
        kb = self.kb
        c, p = self.cols[l], self.pcl[l]
        self.make_rmask(32)
        qT = kb.sb("h_qT", [128, TOK], BF16, st)
        QT = [kb.sb("h_QT%d" % d, [128, TOK], BF16, st) for d in range(2)]
        KT = [kb.sb("h_KT%d" % d, [128, TOK], BF16, st) for d in range(2)]
        KTe = [kb.sb("h_KTe%d" % d, [128, TOK], BF16, st) for d in range(2)]
        vi = kb.sb("h_vi", [128, TB], BF16, st)
        Vtm = kb.sb("h_Vtm", [128, NTT, 128], BF16, st)
        gsil = kb.sb("h_gsil", [128, TOK], BF16, st)
        oacc = [kb.sb("h_oacc%d" % tb, [128, TB], F32, st) for tb in range(NTB)]
        G = {'KT': KT, 'QT': QT, 'KTe': KTe, 'Vtm': Vtm, 'oacc': oacc, 'NV': 128,
             'S32': [[kb.sb("h_S32_%d_%d" % (d, i), [128, 128], F32, st) for i in range(2)] for d in range(2)],
             'e1': [kb.sb("h_e1_%d" % d, [128, 64], F32, st) for d in range(2)],
             'Ktm': [[kb.sb("h_Ktm%d_%d" % (d, i), [128, 128], BF16, st) for i in range(2)] for d in range(2)],
             'ATs': [[kb.sb("h_ATs%d_%d" % (d, i), [128, 128], BF16, st) for i in range(2)] for d in range(2)],
             'mask': [self.cst.t[:, CS_M128F:CS_M128F + 128], self.cst.t[:, CS_M128B:CS_M128B + 128]]}
        hpre = None
        for hd in range(4):
            with ExitStack() as sprep:
                sg = kb.sb("h_sg", [128, TOK], F32, sprep)
                bc = kb.sb("h_bc", [128, TOK], F32, sprep)
                et = kb.sb("h_et", [128, TOK], BF16, sprep)
                kT = kb.sb("h_kT", [128, TOK], BF16, sprep)
                if hpre is not None:
                    wt1, wt2 = hpre
                else:
                    wt1 = self.w_in_cols(l, [C_HQ + hd * 128, C_HFF + hd * 128, C_HFB + hd * 128, C_HI + hd * 128])
                    wt2 = self.w_in_cols(l, [C_HG + hd * 128])
                hpre = None
                lbc = c.t[:, 32 + hd:33 + hd]
                omlc = c.t[:, 36 + hd:37 + hd]
                for tb in range(NTB):
                    sl = slice(tb * TB, (tb + 1) * TB)
                    pt, ps = self.proj(wt1, 0, tb)
                    kb.act(qT.t[:, sl], ps, AF.Copy, pt, [qT], scale=float(128 ** -0.5))
                    pt, ps = self.proj(wt1, 3, tb)
                    kb.cp(vi[:], ps, pt, [vi], eng='dve')
                    for ti in range(4):
                        self.to_tm(Vtm, Vtm.t[:, tb * 4 + ti, :], vi, vi.t[:, ti * 128:(ti + 1) * 128])
                    pt, ps = self.proj(wt2, 0, tb)
                    kb.act(gsil.t[:, sl], ps, AF.Silu, pt, [gsil])
                for d in range(2):
                    for tb in range(NTB):
                        sl = slice(tb * TB, (tb + 1) * TB)
                        pt, ps = self.proj(wt1, 1 + d, tb)
                        kb.act(sg.t[:, sl], ps, AF.Sigmoid, pt, [sg])
                        kb.ts(sg.t[:, sl], sg.t[:, sl], omlc, lbc, ALU.mult, ALU.add, [sg, c], [sg])
                        kb.ts(kT.t[:, sl], sg.t[:, sl], -1.0, 1.0, ALU.mult, ALU.add, [sg], [kT])
                    kb.act(sg[:], sg[:], AF.Ln, [sg], [sg])
                    self.cumsum_chunks(bc, sg, d)
                    bcv = bc.t[:, :].rearrange("p (a b) -> p a b", b=32)
                    bend = bcv[:, :, 31:32] if d == 0 else bcv[:, :, 0:1]
                    kb.act(G['e1'][d].t[:, :].rearrange("p (a b) -> p a b", b=1), bend, AF.Exp, [bc], [G['e1'][d]])
                    kb.act(et[:], bc[:], AF.Exp, [bc], [et])
                    kb.tt(QT[d][:], qT[:], et[:], ALU.mult, [qT, et], [QT[d]])
                    kb.act(et[:], bc[:], AF.Exp, [bc], [et], scale=-1.0)
                    kb.tt(KT[d][:], kT[:], et[:], ALU.mult, [kT, et], [KT[d]])
                    kb.tt(KTe[d].t[:, :].rearrange("p (a b) -> p a b", b=32), KT[d].t[:, :].rearrange("p (a b) -> p a b", b=32),
                          G['e1'][d].t[:, :].rearrange("p (a b) -> p a b", b=1).to_broadcast([128, 64, 32]), ALU.mult,
                          [KT[d], G['e1'][d]], [KTe[d]])
                kb.barrier()
            if hd < 3:
                n_ = hd + 1
                hpre = (self.w_in_cols(l, [C_HQ + n_ * 128, C_HFF + n_ * 128, C_HFB + n_ * 128, C_HI + n_ * 128]),
                        self.w_in_cols(l, [C_HG + n_ * 128]))
            for tb in range(NTB):
                kb.memset(oacc[tb][:], 0.0, [oacc[tb]])
            G['cur'] = [0, 0]
            for d in range(2):
                kb.dma('sp', G['S32'][d][0][:], self.hs0[l, d, hd], writes=[G['S32'][d][0]])

            def seg_out(d, seg, s32, hd=hd):
                kb.out_toks.append(kb.dma('sp', self.so_h[l, d, seg, hd], s32[:], reads=[s32]))
            G['seg_out'] = seg_out
            with ExitStack() as sg_:
                self.gla_run(G, sg_)
                kb.barrier()
            self.head_post(l, oacc, p.t[:, PC_HNORM + hd:PC_HNORM + hd + 1], gsil, self.ycur[hd], vi)

    def branch_mlstm(self, l, st):
        kb = self.kb
        c, p, cs = self.cols[l], self.pcl[l], self.cst
        Z = kb.sb("m_Z", [16, TOK], F32, st)
        mfin = kb.sb("m_mfin", [16, NSEG], F32, st)
        R16 = slice(0, 16)
        mdF, mdB, sgn = cs.t[R16, CS_MDF:CS_MDF + 1], cs.t[R16, CS_MDB:CS_MDB + 1], cs.t[R16, CS_SGN:CS_SGN + 1]
        with ExitStack() as s2:
            A = kb.sb("m_A", [16, TOK], F32, s2)
            B = kb.sb("m_B", [16, TOK], F32, s2)
            C = kb.sb("m_C", [16, TOK], F32, s2)
            Dd = kb.sb("m_D", [16, TOK], F32, s2)
            sm = [kb.sb("m_sm%d" % i, [16, NSEG], F32, s2) for i in range(4)]
            wt = self.w_in_cols(l, [C_MG])
            for tb in range(NTB):
                sl = slice(tb * TB, (tb + 1) * TB)
                pt, ps = self.proj(wt, 0, tb, M=16)
                kb.cp(A.t[:, sl], ps, pt, [A], eng='act')
                pt, ps = kb.ps_big()
                kb.mm(ps[R16, :], cs.t[R16, CS_SELF:CS_SELF + 16], A.t[:, sl], True, True, [cs, A], pt)
                kb.act(B.t[:, sl], ps[R16, :], AF.Sigmoid, pt + [p], [B], bias=p.t[R16, PC_GBF:PC_GBF + 1])
                kb.act(B.t[:, sl], B.t[:, sl], AF.Ln, [B], [B])
                pt, ps = kb.ps_big()
                kb.mm(ps[R16, :], cs.t[R16, CS_SELI:CS_SELI + 16], A.t[:, sl], True, True, [cs, A], pt)
                kb.act(C.t[:, sl], ps[R16, :], AF.Identity, pt + [p], [C], bias=p.t[R16, PC_GBI:PC_GBI + 1])
            self.make_rmask(256)
            for d in range(2):
                self.cumsum_chunks(A, B, d, 16)
                Av = A.t[:, :].rearrange("p (a b) -> p a b", b=256)
                gt_ = Av[:, :, 255:256] if d == 0 else Av[:, :, 0:1]
                kb.cp(sm[d].t[:, :].rearrange("p (a b) -> p a b", b=1), gt_, [A], [sm[d]], eng='dve')
                kb.tt(A[:], C[:], A[:], ALU.subtract, [C, A], [A])
                kb.op('dve', lambda e: e.tensor_reduce(out=sm[2 + d][:], in_=Av, axis=AX.X, op=ALU.max), [A], [sm[2 + d]])
            kb.ts(sm[0][:], sm[0][:], mdF, None, ALU.mult, None, [sm[0], cs], [sm[0]])
            kb.stt(sm[0][:], sm[1][:], mdB, sm[0][:], ALU.mult, ALU.add, [sm[1], cs, sm[0]], [sm[0]])
            kb.ts(sm[2][:], sm[2][:], mdF, None, ALU.mult, None, [sm[2], cs], [sm[2]])
            kb.stt(sm[2][:], sm[3][:], mdB, sm[2][:], ALU.mult, ALU.add, [sm[3], cs, sm[2]], [sm[2]])
            kb.ts(sm[2][:], sm[2][:], 0.0, None, ALU.max, None, [sm[2]], [sm[2]])
            kb.tt(mfin[:], sm[0][:], sm[2][:], ALU.add, [sm[0], sm[2]], [mfin])
            kb.out_toks.append(kb.dma('sp', self.so_m[l], mfin[:], reads=[mfin]))
            self.make_rmask(32)
            self.cumsum_chunks(A, B, 0, 16)
            self.cumsum_chunks(Dd, B, 1, 16)
            kb.ts(A[:], A[:], mdF, None, ALU.mult, None, [A, cs], [A])
            kb.stt(A[:], Dd[:], mdB, A[:], ALU.mult, ALU.add, [Dd, cs, A], [A])
            kb.stt(Z[:], A[:], sgn, C[:], ALU.mult, ALU.add, [A, cs, C], [Z])
            kb.barrier()
        QT = [kb.sb("m_QT%d" % d, [128, TOK], BF16, st) for d in range(2)]
        KT = [kb.sb("m_KT%d" % d, [128, TOK], BF16, st) for d in range(2)]
        KTe = [kb.sb("m_KTe%d" % d, [128, TOK], BF16, st) for d in range(2)]
        vi = kb.sb("m_vi", [128, TB], BF16, st)
        Vtm = kb.sb("m_Vtm", [128, NTT, 130], BF16, st)
        osig = kb.sb("m_osig", [128, TOK], BF16, st)
        oacc = [kb.sb("m_oacc%d" % tb, [128, TB], F32, st) for tb in range(NTB)]
        sr = kb.sb("m_sr", [16, 2, 128], F32, st)
        emf = [kb.sb("m_emf%d" % d, [128, NSEG], F32, st) for d in range(2)]
        em0 = kb.sb("m_em0", [128, 1], F32, st)
        stage = [kb.sb("m_stage%d" % i, [128, 129], F32, st) for i in range(2)]
        G = {'KT': KT, 'QT': QT, 'KTe': KTe, 'Vtm': Vtm, 'oacc': oacc, 'NV': 129, 'den': True,
             'S32': [[kb.sb("m_S32_%d_%d" % (d, i), [128, 129], F32, st) for i in range(2)] for d in range(2)],
             'e1': [kb.sb("m_e1_%d" % d, [128, 64], F32, st) for d in range(2)],
             'Ktm': [[kb.sb("m_Ktm%d_%d" % (d, i), [128, 128], BF16, st) for i in range(2)] for d in range(2)],
             'ATs': [[kb.sb("m_ATs%d_%d" % (d, i), [128, 128], BF16, st) for i in range(2)] for d in range(2)],
             'mask': [self.cst.t[:, CS_M128F:CS_M128F + 128], self.cst.t[:, CS_M128B:CS_M128B + 128]]}
        kb.memset(Vtm.t[:, :, 128:129], 1.0, [Vtm])
        stg_i = [0]
        mpre = None
        for hd in range(4):
            with ExitStack() as sprep:
                qT = kb.sb("m_qT", [128, TOK], BF16, sprep)
                kT = kb.sb("m_kT", [128, TOK], BF16, sprep)
                eg = [kb.sb("m_eg%d" % i, [128, TB], F32, sprep) for i in range(2)]
                if mpre is not None:
                    wt1 = mpre
                else:
                    wt1 = self.w_in_cols(l, [C_MQ + hd * 128, C_MK + hd * 128, C_MV + hd * 128, C_MO + hd * 128])
                mpre = None
                for tb in range(NTB):
                    sl = slice(tb * TB, (tb + 1) * TB)
                    pt, ps = self.proj(wt1, 0, tb)
                    kb.cp(qT.t[:, sl], ps, pt, [qT], eng='act')
                    pt, ps = self.proj(wt1, 1, tb)
                    kb.act(kT.t[:, sl], ps, AF.Copy, pt, [kT], scale=float(128 ** -0.5))
                    pt, ps = self.proj(wt1, 2, tb)
                    kb.cp(vi[:], ps, pt, [vi], eng='dve')
                    for ti in range(4):
                        self.to_tm(Vtm, Vtm.t[:, tb * 4 + ti, 0:128], vi, vi.t[:, ti * 128:(ti + 1) * 128])
                    pt, ps = self.proj(wt1, 3, tb)
                    kb.act(osig.t[:, sl], ps, AF.Sigmoid, pt, [osig])
                for d in range(2):
                    p_ = d * 4 + hd
                    kb.ts(sr.t[:, 0, :], cs.t[R16, CS_ONES:CS_ONES + 128], cs.t[R16, CS_ID + p_:CS_ID + p_ + 1], None, ALU.mult, None, [cs], [sr])
                    kb.ts(sr.t[:, 1, :], cs.t[R16, CS_ONES:CS_ONES + 128], cs.t[R16, CS_ID + 8 + p_:CS_ID + 9 + p_], None, ALU.mult, None, [cs], [sr])
                    for tb in range(NTB):
                        sl = slice(tb * TB, (tb + 1) * TB)
                        pt, ps = kb.ps_big()
                        kb.mm(ps, sr.t[:, 0, :], Z.t[:, sl], True, True, [sr, Z], pt)
                        e_ = eg[0]
                        kb.act(e_[:], ps, AF.Exp, pt, [e_])
                        kb.tt(QT[d].t[:, sl], qT.t[:, sl], e_[:], ALU.mult, [qT, e_], [QT[d]])
                        ev = e_.t[:, :].rearrange("p (a b) -> p a b", b=32)
                        kb.cp(G['e1'][d].t[:, tb * 16:(tb + 1) * 16].rearrange("p (a b) -> p a b", b=1), ev[:, :, 31:32] if d == 0 else ev[:, :, 0:1],
                              [e_], [G['e1'][d]], eng='dve')
                        pt, ps = kb.ps_big()
                        kb.mm(ps, sr.t[:, 1, :], Z.t[:, sl], True, True, [sr, Z], pt)
                        e2_ = eg[1]
                        kb.act(e2_[:], ps, AF.Exp, pt, [e2_])
                        kb.tt(KT[d].t[:, sl], kT.t[:, sl], e2_[:], ALU.mult, [kT, e2_], [KT[d]])
                    kb.tt(KTe[d].t[:, :].rearrange("p (a b) -> p a b", b=32), KT[d].t[:, :].rearrange("p (a b) -> p a b", b=32),
                          G['e1'][d].t[:, :].rearrange("p (a b) -> p a b", b=1).to_broadcast([128, 64, 32]), ALU.mult,
                          [KT[d], G['e1'][d]], [KTe[d]])
                    pt, ps = kb.ps_small()
                    kb.mm(ps[:, 0:NSEG], sr.t[:, 1, :], mfin[:], True, True, [sr, mfin], pt)
                    kb.act(emf[d][:], ps[:, 0:NSEG], AF.Exp, pt, [emf[d]], scale=-1.0)
                    s0 = G['S32'][d][0]
                    kb.dma('sp', s0.t[:, 0:128], self.cs0[l, d, hd], writes=[s0])
                    kb.dma('sp', s0.t[:, 128:129], self.ns0[l, d, hd], writes=[s0])
                    mcol = self.cc.t[:, CC_M0 + l * 8 + d * 4 + hd:CC_M0 + l * 8 + d * 4 + hd + 1]
                    kb.act(em0[:], mcol, AF.Exp, [self.cc], [em0])
                    kb.ts(s0[:], s0[:], em0[:], None, ALU.mult, None, [s0, em0], [s0])
                kb.barrier()
            if hd < 3:
                n_ = hd + 1
                mpre = self.w_in_cols(l, [C_MQ + n_ * 128, C_MK + n_ * 128, C_MV + n_ * 128, C_MO + n_ * 128])
            for tb in range(NTB):
                kb.memset(oacc[tb][:], 0.0, [oacc[tb]])
            G['cur'] = [0, 0]

            def seg_out(d, seg, s32, hd=hd):
                sg_ = stage[stg_i[0]]
                stg_i[0] ^= 1
                kb.act(sg_[:], s32[:], AF.Copy, [s32, emf[d]], [sg_], scale=emf[d].t[:, seg:seg + 1])
                kb.out_toks.append(kb.dma('sp', self.so_c[l, d, seg, hd], sg_.t[:, 0:128], reads=[sg_]))
                kb.out_toks.append(kb.dma('sp', self.so_n[l, d, seg, hd], sg_.t[:, 128:129], reads=[sg_]))
            G['seg_out'] = seg_out
            with ExitStack() as sg_:
                self.gla_run(G, sg_)
                kb.barrier()
            self.head_post(l, oacc, p.t[:, PC_MNORM + hd:PC_MNORM + hd + 1], osig, self.ycur[hd], vi)

    def shift_chunk(self, l, cidx, raw, sh, tmpb, nxt=None):
        kb = self.kb
        me = self.mue[l]
        pre = getattr(self, '_shift_pre', None)
        if pre is not None and pre[0] == (l, cidx):
            wt = pre[1]
        else:
            wt = self.w_in_cols(l, [C_RW + cidx * 128])
        self._shift_pre = None
        if nxt is not None:
            self._shift_pre = ((l, nxt), self.w_in_cols(l, [C_RW + nxt * 128]))
        for tb in range(NTB):
            pt, ps = self.proj(wt, 0, tb)
            kb.cp(raw.t[:, tb * TB:(tb + 1) * TB], ps, pt, [raw], eng='act' if tb % 2 else 'dve')
        mu = [me.t[:, j * 15 + cidx:j * 15 + cidx + 1] for j in range(4)]
        c0 = me.t[:, 60 + cidx:61 + cidx]
        kb.act(sh[:], raw[:], AF.Copy, [raw, me], [sh], scale=c0)
        rv = raw.t[:, :].rearrange("p (a b) -> p a b", b=64)
        sv = sh.t[:, :].rearrange("p (a b) -> p a b", b=64)
        R, W = [raw, me, sh], [sh]
        kb.stt(sv[:, :, 1:64], rv[:, :, 0:63], mu[0], sv[:, :, 1:64], ALU.mult, ALU.add, R, W)
        kb.stt(sv[:, :, 0:63], rv[:, :, 1:64], mu[1], sv[:, :, 0:63], ALU.mult, ALU.add, R, W)
        kb.stt(sh.t[:, 64:TOK], raw.t[:, 0:TOK - 64], mu[2], sh.t[:, 64:TOK], ALU.mult, ALU.add, R, W)
        kb.stt(sh.t[:, 0:TOK - 64], raw.t[:, 64:TOK], mu[3], sh.t[:, 0:TOK - 64], ALU.mult, ALU.add, R, W)
        tb3 = tmpb.t[:, :].rearrange("p (a b) -> p a b", b=1)
        bml = self.cc.t[:, CC_BML:CC_BML + 31].rearrange("p (a b) -> p a b", b=1)
        bmr = self.cc.t[:, CC_BMR:CC_BMR + 31].rearrange("p (a b) -> p a b", b=1)
        kb.tt(tb3, rv[:, 0:31, 63:64], bml, ALU.mult, [raw, self.cc], [tmpb])
        kb.stt(sv[:, 1:32, 0:1], tb3, mu[0], sv[:, 1:32, 0:1], ALU.mult, ALU.add, [tmpb, me, sh], W)
        kb.tt(tb3, rv[:, 1:32, 0:1], bmr, ALU.mult, [raw, self.cc], [tmpb])
        kb.stt(sv[:, 0:31, 63:64], tb3, mu[1], sv[:, 0:31, 63:64], ALU.mult, ALU.add, [tmpb, me, sh], W)

    def branch_rwkv(self, l, st):
        kb = self.kb
        c, p, cs = self.cols[l], self.pcl[l], self.cst
        self.make_rmask(128)
        twT = kb.sb("r_twT", [128, TOK], BF16, st)
        aloT = kb.sb("r_aloT", [128, TOK], BF16, st)
        sglo = kb.sb("r_sglo", [128, TOK], BF16, st)
        W2 = kb.sb("r_W2", [128, 512], BF16, st)
        A2 = kb.sb("r_A2", [128, 512], BF16, st)
        G2 = kb.sb("r_G2", [128, 512], BF16, st)
        omka = kb.sb("r_omka", [128, 4], F32, st)
        tmpb = kb.sb("r_tmpb", [128, 31], F32, st)
        for d in range(2):
            kb.dma('pool', W2.t[64 * d:64 * d + 64, :], self.w2[l, d], writes=[W2])
            kb.dma('pool', A2.t[64 * d:64 * d + 64, :], self.a2[l, d], writes=[A2])
        kb.dma('pool', G2[:], self.g2[l], writes=[G2])
        kb.ts(omka[:], p.t[:, PC_KA:PC_KA + 4], -1.0, 1.0, ALU.mult, ALU.add, [p], [omka])
        with ExitStack() as s0:
            raw = kb.sb("r_raw", [128, TOK], F32, s0)
            sh = kb.sb("r_sh", [128, TOK], F32, s0)
            self.shift_chunk(l, 12, raw, sh, tmpb, nxt=13)
            kb.act(twT[:], sh[:], AF.Tanh, [sh], [twT])
            self.shift_chunk(l, 13, raw, sh, tmpb, nxt=14)
            kb.cp(aloT[:], sh[:], [sh], [aloT], eng='act')
            self.shift_chunk(l, 14, raw, sh, tmpb, nxt=0)
            kb.act(sglo[:], sh[:], AF.Sigmoid, [sh], [sglo])
            kb.barrier()
        rwm = [cs.t[:, CS_RWF:CS_RWF + 256], cs.t[:, CS_RWB:CS_RWB + 256]]
        strict_st = [cs.t[:, CS_RWF:CS_RWF + 128], cs.t[:, CS_RWB:CS_RWB + 128]]
        strict_ts = [cs.t[:, CS_RWB:CS_RWB + 128], cs.t[:, CS_RWF:CS_RWF + 128]]
        ident_b = self.ident_b
        for pr in range(4):
            with ExitStack() as sp:
                rT = kb.sb("r_rT", [128, TOK], BF16, sp)
                kT = kb.sb("r_kT", [128, TOK], BF16, sp)
                kkT = kb.sb("r_kkT", [128, TOK], BF16, sp)
                Vtm = kb.sb("r_Vtm", [128, NTT, 128], BF16, sp)
                bon = kb.sb("r_bon", [128, TOK], BF16, sp)
                oacc = [kb.sb("r_oacc%d" % tb, [128, TB], F32, sp) for tb in range(NTB)]
                kkc = p.t[:, PC_KK + pr:PC_KK + pr + 1]
                rkc = p.t[:, PC_RK + pr:PC_RK + pr + 1]
                kac = p.t[:, PC_KA + pr:PC_KA + pr + 1]
                omkac = omka.t[:, pr:pr + 1]
                with ExitStack() as s1:
                    raw = kb.sb("r_raw", [128, TOK], F32, s1)
                    sh = kb.sb("r_sh", [128, TOK], F32, s1)
                    vT = kb.sb("r_vT", [128, TOK], BF16, s1)
                    t5 = [kb.sb("r_t5_%d" % i, [128, TB], F32, s1) for i in range(2)]
                    b5 = [kb.sb("r_b5_%d" % i, [128, TB], BF16, s1) for i in range(2)]
                    self.shift_chunk(l, pr, raw, sh, tmpb, nxt=4 + pr)
                    kb.cp(rT[:], sh[:], [sh], [rT], eng='act')
                    self.shift_chunk(l, 4 + pr, raw, sh, tmpb, nxt=8 + pr)
                    kb.cp(kT[:], sh[:], [sh], [kT], eng='act')
                    for tb in range(NTB):
                        sl = slice(tb * TB, (tb + 1) * TB)
                        kr = t5[tb % 2]
                        kb.ts(kr[:], sh.t[:, sl], kkc, None, ALU.mult, None, [sh, p], [kr])
                        sq = b5[tb % 2]
                        kb.act(sq[:], kr[:], AF.Square, [kr], [sq])
                        pt, ps = kb.ps_big()
                        kb.mm(ps, self.bones_b, sq[:], True, True, [self.cstb, sq], pt)
                        nr = self.rstd[tb % 2]
                        kb.act(nr[:], ps, AF.Sqrt, pt, [nr])
                        kb.ts(nr[:], nr[:], 1e-12, None, ALU.max, None, [nr], [nr])
                        kb.op('dve', lambda e: e.reciprocal(out=nr[:], in_=nr[:]), [nr], [nr])
                        kb.tt(kkT.t[:, sl], kr[:], nr[:], ALU.mult, [kr, nr], [kkT])
                    self.shift_chunk(l, 8 + pr, raw, sh, tmpb, nxt=(pr + 1) if pr < 3 else None)
                    kb.cp(vT[:], sh[:], [sh], [vT], eng='act')
                    for ti in range(NTT):
                        self.to_tm(Vtm, Vtm.t[:, ti, :], vT, vT.t[:, ti * 128:(ti + 1) * 128], eng='act' if ti % 2 else 'dve')
                    for tb in range(NTB):
                        sl = slice(tb * TB, (tb + 1) * TB)
                        bk = b5[tb % 2]
                        kb.stt(bk[:], rT.t[:, sl], rkc, kT.t[:, sl], ALU.mult, ALU.mult, [rT, p, kT], [bk])
                        pt, ps = kb.ps_big()
                        kb.mm(ps, self.bones_b, bk[:], True, True, [self.cstb, bk], pt)
                        kb.tt(bon.t[:, sl], ps, vT.t[:, sl], ALU.mult, pt + [vT], [bon])
                    kb.barrier()
                for tb in range(NTB):
                    kb.memset(oacc[tb][:], 0.0, [oacc[tb]])
                for d in range(2):
                    with ExitStack() as sd:
                        AR = kb.sb("r_AR", [128, NTT, 2, 128], BF16, sd)
                        BT = kb.sb("r_BT", [128, TOK], BF16, sd)
                        KTt = kb.sb("r_KTt", [128, TOK], BF16, sd)
                        e1 = kb.sb("r_e1", [128, NTT], F32, sd)
                        e2 = kb.sb("r_e2", [128, NTT], F32, sd)
                        e3 = kb.sb("r_e3", [128, NTT], F32, sd)
                        w0c = p.t[:, PC_W0 + 4 * d + pr:PC_W0 + 4 * d + pr + 1]
                        a0c = p.t[:, PC_A0 + 4 * d + pr:PC_A0 + 4 * d + pr + 1]
                        R64 = slice(64 * d, 64 * d + 64)
                        with ExitStack() as s2:
                            sw = kb.sb("r_sw", [128, TB], F32, s2)
                            cw = kb.sb("r_cw", [128, TB], F32, s2)
                            at = kb.sb("r_at", [128, TB], F32, s2)
                            km = kb.sb("r_km", [128, TB], F32, s2)
                            kd = kb.sb("r_kd", [128, TB], BF16, s2)
                            bt = kb.sb("r_bt", [128, TB], BF16, s2)
                            et = kb.sb("r_et", [128, TB], BF16, s2)
                            cmid = kb.sb("r_cmid", [128, 4], F32, s2)
                            ctmp = kb.sb("r_ctmp", [128, 4], F32, s2)
                            for tb in range(NTB):
                                sl = slice(tb * TB, (tb + 1) * TB)
                                csl = slice(tb * 4, tb * 4 + 4)
                                pt, ps = kb.ps_big()
                                kb.mm(ps, W2.t[R64, pr * 128:(pr + 1) * 128], twT.t[R64, sl], True, True, [W2, twT], pt)
                                kb.act(sw[:], ps, AF.Sigmoid, pt + [p], [sw], bias=w0c)
                                pt, ps = kb.ps_big()
                                kb.mm(ps, A2.t[R64, pr * 128:(pr + 1) * 128], aloT.t[R64, sl], True, True, [A2, aloT], pt)
                                kb.act(at[:], ps, AF.Sigmoid, pt + [p], [at], bias=a0c)
                                kb.ts(km[:], at[:], kac, omkac, ALU.mult, ALU.add, [at, p, omka], [km])
                                kb.tt(kd[:], kT.t[:, sl], km[:], ALU.mult, [kT, km], [kd])
                                kb.tt(bt[:], kkT.t[:, sl], at[:], ALU.mult, [kkT, at], [bt])
                                if d == 0:
                                    kb.op('dve', lambda e: e.tensor_tensor_scan(out=cw[:], data0=self.rmF.t[:, :], data1=sw[:], initial=0.0,
                                                                                op0=ALU.mult, op1=ALU.add), [self.rmF, sw], [cw])
                                else:
                                    kb.op('dve', lambda e: e.tensor_tensor_scan(out=cw.t[:, ::-1], data0=self.rmB.t[:, ::-1], data1=sw.t[:, ::-1],
                                                                                initial=0.0, op0=ALU.mult, op1=ALU.add), [self.rmB, sw], [cw])
                                cv = cw.t[:, :].rearrange("p (a b) -> p a b", b=128)
                                cend = cv[:, :, 127:128] if d == 0 else cv[:, :, 0:1]
                                cm3 = cmid.t[:, :].rearrange("p (a b) -> p a b", b=1)
                                ct3 = ctmp.t[:, :].rearrange("p (a b) -> p a b", b=1)
                                kb.cp(cm3, cv[:, :, 63:64], [cw], [cmid], eng='dve')
                                kb.act(e1.t[:, csl].rearrange("p (a b) -> p a b", b=1), cend, AF.Exp, [cw], [e1], scale=-KAPPA)
                                kb.act(e3.t[:, csl], cmid[:], AF.Exp, [cmid], [e3], scale=-KAPPA)
                                kb.tt(ct3, cend, cm3, ALU.subtract, [cw, cmid], [ctmp])
                                kb.act(e2.t[:, csl], ctmp[:], AF.Exp, [ctmp], [e2], scale=-KAPPA)
                                kb.tt(sw[:], cw[:], sw[:], ALU.subtract, [cw, sw], [sw])
                                swv = sw.t[:, :].rearrange("p (a b) -> p a b", b=128)
                                kb.tt(swv, swv, cm3.to_broadcast([128, 4, 128]), ALU.subtract, [sw, cmid], [sw])
                                kb.tt(cv, cv, cm3.to_broadcast([128, 4, 128]), ALU.subtract, [cw, cmid], [cw])
                                arv = AR.t[:, tb * 4:(tb + 1) * 4, :, :]
                                kb.act(et[:], cw[:], AF.Exp, [cw], [et], scale=-KAPPA)
                                kb.tt(arv[:, :, 1, :], rT.t[:, sl].rearrange("p (a b) -> p a b", b=128), et.t[:, :].rearrange("p (a b) -> p a b", b=128),
                                      ALU.mult, [rT, et], [AR])
                                kb.act(et[:], cw[:], AF.Exp, [cw], [et], scale=KAPPA)
                                kb.tt(BT.t[:, sl], bt[:], et[:], ALU.mult, [bt, et], [BT])
                                kb.tt(KTt.t[:, sl], kd[:], et[:], ALU.mult, [kd, et], [KTt])
                                kb.act(et[:], sw[:], AF.Exp, [sw], [et], scale=-KAPPA)
                                kb.tt(arv[:, :, 0, :], kkT.t[:, sl].rearrange("p (a b) -> p a b", b=128), et.t[:, :].rearrange("p (a b) -> p a b", b=128),
                                      ALU.mult, [kkT, et], [AR])
                            kb.barrier()
                        self.rwkv_chain(l, d, pr, AR, BT, KTt, e1, e2, e3, Vtm, oacc, rwm[d], strict_st[d], strict_ts[d], sd)
                        kb.barrier()
                with ExitStack() as s3:
                    ob = kb.sb("r_ob", [128, 2, TB], BF16, s3)
                    mean = kb.sb("r_mean", [128, TB], F32, s3)
                    var = kb.sb("r_var", [128, TB], F32, s3)
                    cen = kb.sb("r_cen", [128, TB], F32, s3)
                    ln0 = p.t[:, PC_LN0 + pr:PC_LN0 + pr + 1]
                    ln1 = p.t[:, PC_LN1 + pr:PC_LN1 + pr + 1]
                    for tb in range(NTB):
                        sl = slice(tb * TB, (tb + 1) * TB)
                        o = oacc[tb]
                        kb.cp(ob.t[:, 0, :], o[:], [o], [ob], eng='act')
                        kb.act(ob.t[:, 1, :], o[:], AF.Square, [o], [ob])
                        ptm, psm = kb.ps_big()
                        kb.mm(psm, self.bones_b, ob.t[:, 0, :], True, True, [self.cstb, ob], ptm)
                        pts, pss = kb.ps_big()
                        kb.mm(pss, self.bones_b, ob.t[:, 1, :], True, True, [self.cstb, ob], pts)
                        kb.act(mean[:], psm, AF.Copy, ptm, [mean], scale=1.0 / 64)
                        kb.tt(var[:], mean[:], mean[:], ALU.mult, [mean], [var])
                        kb.stt(var[:], pss, 1.0 / 64, var[:], ALU.mult, ALU.subtract, pts + [var], [var])
                        kb.act(var[:], var[:], AF.Sqrt, [var], [var], bias=64e-5, scale=1.0)
                        kb.op('dve', lambda e: e.reciprocal(out=var[:], in_=var[:]), [var], [var])
                        kb.tt(cen[:], o[:], mean[:], ALU.subtract, [o, mean], [cen])
                        kb.stt(cen[:], cen[:], ln0, var[:], ALU.mult, ALU.mult, [cen, p, var], [cen])
                        kb.stt(cen[:], cen[:], ln1, bon.t[:, sl], ALU.add, ALU.add, [cen, p, bon], [cen])
                        ptg, psg = kb.ps_big()
                        kb.mm(psg, G2.t[:, pr * 128:(pr + 1) * 128], sglo.t[:, sl], True, True, [G2, sglo], ptg)
                        kb.tt(self.ycur[pr].t[:, sl], cen[:], psg, ALU.mult, [cen] + ptg, [self.ycur[pr]])
                    kb.barrier()

    def rwkv_chain(self, l, d, pr, AR, BT, KTt, e1, e2, e3, Vtm, oacc, rwm, m_st, m_ts, st):
        kb = self.kb
        ident_b = self.ident_b
        NPB = 3
        NA = [[kb.sb("r_NA%d_%d" % (hp, i), [128, 128], BF16, st) for i in range(NPB)] for hp in range(2)]
        NK = [[kb.sb("r_NK%d_%d" % (hp, i), [128, 256], BF16, st) for i in range(NPB)] for hp in range(2)]
        TT = [[kb.sb("r_TT%d_%d" % (hp, i), [128, 128], BF16, st) for i in range(NPB)] for hp in range(2)]
        Xb = [[[kb.sb("r_X%d_%d_%d" % (hp, i, u), [128, 128], BF16, st) for i in range(2)] for hp in range(2)] for u in range(NPB)]
        XTb = [[[kb.sb("r_XT%d_%d_%d" % (hp, i, u), [128, 128], BF16, st) for i in range(2)] for hp in range(2)] for u in range(NPB)]
        Btm = [kb.sb("r_Btm%d" % i, [128, 128], BF16, st) for i in range(NPB)]
        Ktm = [kb.sb("r_Ktm%d" % i, [128, 128], BF16, st) for i in range(NPB)]
        Mz = [kb.sb("r_Mz%d" % hp, [128, 64], BF16, st) for hp in range(2)]
        nW = [kb.sb("r_nW%d" % hp, [128, 64], BF16, st) for hp in range(2)]
        U = [kb.sb("r_U%d" % hp, [128, 64], BF16, st) for hp in range(2)]
        M32 = [kb.sb("r_M32_%d" % i, [128, 64], F32, st) for i in range(2)]
        Pe = kb.sb("r_Pe", [128, 64], F32, st)
        for hp in range(2):
            kb.memset(Mz[hp][:], 0.0, [Mz[hp]])
        cur = 0
        for hp in range(2):
            kb.dma('sp', M32[0].t[64 * hp:64 * hp + 64, :], self.rs0[l, d, 2 * pr + hp], writes=[M32[0]])
        order = list(range(NTT)) if d == 0 else list(range(NTT - 1, -1, -1))

        def par(tile, bi):
            tsl = slice(tile * 128, (tile + 1) * 128)
            self.to_tm(Btm[bi], Btm[bi][:], BT, BT.t[:, tsl], eng='act')
            self.to_tm(Ktm[bi], Ktm[bi][:], KTt, KTt.t[:, tsl], eng='dve')
            yield
            pss_ = []
            for hp in range(2):
                H = slice(64 * hp, 64 * hp + 64)
                arf = AR.t[H, tile, :, :].rearrange("p a b -> p (a b)")
                pt1, ps1 = kb.ps_small()
                kb.mm(ps1, BT.t[H, tsl], arf, True, True, [BT, AR], pt1)
                pt2, ps2 = kb.ps_small()
                kb.mm(ps2, KTt.t[H, tsl], arf, True, True, [KTt, AR], pt2)
                pt3, ps3 = kb.ps_small()
                kb.mm(ps3[:, 0:128], AR.t[H, tile, 0, :], BT.t[H, tsl], True, True, [AR, BT], pt3)
                pss_.append((pt1, ps1, pt2, ps2, pt3, ps3))
            yield
            cur_x = []
            for hp in range(2):
                pt1, ps1, pt2, ps2, pt3, ps3 = pss_[hp]
                na, nk, tt_ = NA[hp][bi], NK[hp][bi], TT[hp][bi]
                X, XT = Xb[bi][hp][0], XTb[bi][hp][0]
                kb.stt(X[:], ps3[:, 0:128], -1.0, m_ts, ALU.mult, ALU.mult, pt3 + [self.cst], [X])
                kb.stt(XT[:], ps1[:, 0:128], -1.0, m_st, ALU.mult, ALU.mult, pt1 + [self.cst], [XT])
                kb.tt(na[:], ps1[:, 128:256], rwm[:, 128:256], ALU.mult, pt1 + [self.cst], [na])
                kb.tt(nk[:], ps2, rwm, ALU.mult, pt2 + [self.cst], [nk])
                kb.tt(tt_[:], XT[:], ident_b, ALU.add, [XT, self.cstb], [tt_], eng='pool')
                cur_x.append((X, XT))
            yield
            pend = [None, None]
            for r in range(6):
                pp = []
                for hp in range(2):
                    X, XT = cur_x[hp]
                    tt_ = TT[hp][bi]
                    pa, psa = kb.ps_small()
                    kb.mm(psa[:, 0:128], XT[:], X[:], True, True, [XT, X], pa)
                    pb = psb = None
                    if r < 5:
                        pb, psb = kb.ps_small()
                        kb.mm(psb[:, 0:128], X[:], XT[:], True, True, [X, XT], pb)
                    pp.append((pa, psa, pb, psb))
                yield
                for hp in range(2):
                    pa, psa, pb, psb = pp[hp]
                    X2, XT2 = Xb[bi][hp][(r + 1) % 2], XTb[bi][hp][(r + 1) % 2]
                    kb.cp(X2[:], psa[:, 0:128], pa, [X2], eng='act')
                    if r < 5:
                        kb.cp(XT2[:], psb[:, 0:128], pb, [XT2], eng='dve')
                    cur_x[hp] = (X2, XT2)
                yield
                pcs = []
                for hp in range(2):
                    X2 = cur_x[hp][0]
                    tt_ = TT[hp][bi]
                    pc_, psc = kb.ps_small()
                    kb.mm(psc[:, 0:128], X2[:], tt_[:], True, True, [X2, tt_], pc_)
                    pcs.append((pc_, psc))
                yield
                for hp in range(2):
                    tt_ = TT[hp][bi]
                    pc_, psc = pcs[hp]
                    kb.tt(tt_[:], tt_[:], psc[:, 0:128], ALU.add, [tt_] + pc_, [tt_])
                yield

        def seq(tile, bi):
            nonlocal cur
            m32 = M32[cur]
            for hp in range(2):
                H = slice(64 * hp, 64 * hp + 64)
                kb.act(Mz[hp].t[H, :], m32.t[H, :], AF.Copy, [m32, e3], [Mz[hp]], scale=e3.t[H, tile:tile + 1])
            yield
            pw = []
            for hp in range(2):
                V_h = Vtm.t[:, tile, 64 * hp:64 * hp + 64]
                pt, ps = kb.ps_small()
                kb.mm(ps[:, 0:64], AR.t[:, tile, 0, :], Mz[hp][:], True, False, [AR, Mz[hp]], pt)
                kb.mm(ps[:, 0:64], NK[hp][bi].t[:, 0:128], V_h, False, True, [NK[hp][bi], Vtm], pt)
                pw.append((pt, ps))
            yield
            for hp in range(2):
                pt, ps = pw[hp]
                kb.act(nW[hp][:], ps[:, 0:64], AF.Copy, pt, [nW[hp]], scale=-1.0)
            yield
            pu = []
            for hp in range(2):
                pt, ps = kb.ps_small()
                kb.mm(ps[:, 0:64], TT[hp][bi][:], nW[hp][:], True, True, [TT[hp][bi], nW[hp]], pt)
                pu.append((pt, ps))
            yield
            for hp in range(2):
                pt, ps = pu[hp]
                kb.cp(U[hp][:], ps[:, 0:64], pt, [U[hp]], eng='dve')
            yield
            pts, pss = kb.ps_small()
            for hp in range(2):
                H = slice(64 * hp, 64 * hp + 64)
                V_h = Vtm.t[:, tile, 64 * hp:64 * hp + 64]
                kb.mm(pss[H, 0:64], Btm[bi].t[:, H], U[hp][:], True, False, [Btm[bi], U[hp]], pts)
                kb.mm(pss[H, 0:64], Ktm[bi].t[:, H], V_h, False, True, [Ktm[bi], Vtm], pts)
            pto, pso = kb.ps_small()
            for hp in range(2):
                H = slice(64 * hp, 64 * hp + 64)
                V_h = Vtm.t[:, tile, 64 * hp:64 * hp + 64]
                kb.mm(pso[H, 0:128], Mz[hp][:], AR.t[:, tile, 1, :], True, False, [Mz[hp], AR], pto)
                kb.mm(pso[H, 0:128], U[hp][:], NA[hp][bi].t[:, 0:128], False, False, [U[hp], NA[hp][bi]], pto)
                kb.mm(pso[H, 0:128], V_h, NK[hp][bi].t[:, 128:256], False, True, [Vtm, NK[hp][bi]], pto)
            yield
            kb.act(Pe[:], pss[:, 0:64], AF.Copy, pts + [e2], [Pe], scale=e2.t[:, tile:tile + 1])
            kb.stt(m32[:], m32[:], e1.t[:, tile:tile + 1], Pe[:], ALU.mult, ALU.add, [m32, e1, Pe], [m32])
            ob = oacc[tile // 4]
            osl = slice((tile % 4) * 128, (tile % 4) * 128 + 128)
            kb.tt(ob.t[:, osl], ob.t[:, osl], pso[:, 0:128], ALU.add, [ob] + pto, [ob])
            seg_end = (d == 0 and tile % 2 == 1) or (d == 1 and tile % 2 == 0)
            if seg_end:
                seg = tile // 2
                for hp in range(2):
                    kb.out_toks.append(kb.dma('sp', self.so_r[l, d, seg, 2 * pr + hp], m32.t[64 * hp:64 * hp + 64, :], reads=[m32]))
                nxt = M32[1 - cur]
                kb.ts(nxt[:], m32[:], self.cc.t[:, CC_KEEP:CC_KEEP + 1], None, ALU.mult, None, [m32, self.cc], [nxt])
                cur = 1 - cur
            yield

        def step(g):
            try:
                next(g)
                return True
            except StopIteration:
                return False

        pars = {}
        for j in range(min(NPB - 1, NTT)):
            pars[j] = par(order[j], j % NPB)
        nxt_par = min(NPB - 1, NTT)
        for i, tile in enumerate(order):
            if i in pars:
                while step(pars[i]):
                    pass
                del pars[i]
            if nxt_par < NTT:
                pars[nxt_par] = par(order[nxt_par], nxt_par % NPB)
                nxt_par += 1
            g = seq(tile, i % NPB)
            alive = True
            while alive:
                alive = step(g)
                for j in sorted(pars):
                    for _ in range(3):
                        if not step(pars[j]):
                            del pars[j]
                            break

    def merge(self, l, i, first):
        kb = self.kb
        wb = self.wbt
        kb.dma('pool', wb.t[:, :].rearrange("p (c m) -> p c m", m=1024), self.w_branch[l, i].rearrange("(c p) m -> p c m", p=128), writes=[wb])
        wbv = wb.t[:, :].rearrange("p (c m) -> p c m", m=1024)
        for half in range(2):
            wg = self.w_in_cols(l, [C_GATES + i * 1024 + half * 512 + j * 128 for j in range(4)])
            for mm_ in range(4):
                m = half * 4 + mm_
                for tb in range(NTB):
                    ptg, pg = self.proj(wg, mm_, tb)
                    gate = self.rstd[(m * NTB + tb) % 2]
                    kb.act(gate[:], pg, AF.Sigmoid, ptg, [gate])
                    pt, ps = kb.ps_big()
                    for cch in range(4):
                        kb.mm(ps, wbv[:, cch, m * 128:(m + 1) * 128], self.ycur[cch].t[:, tb * TB:(tb + 1) * TB], cch == 0, cch == 3,
                              [wb, self.ycur[cch]], pt)
                    mg = self.merged[tb]
                    if first:
                        kb.tt(mg.t[:, m, :], ps, gate[:], ALU.mult, pt + [gate], [mg])
                    else:
                        tmp = self.tmpA[self.ta_i]
                        self.ta_i ^= 1
                        kb.tt(tmp[:], ps, gate[:], ALU.mult, pt + [gate], [tmp])
                        kb.tt(mg.t[:, m, :], mg.t[:, m, :], tmp[:], ALU.add, [mg, tmp], [mg], eng='pool')


def _cols(v, n):
    return np.ascontiguousarray(np.asarray(v, np.float32).reshape(n, 128).T)


def _build_consts():
    c = np.zeros((128, NCST), np.float32)
    c[:, CS_ID:CS_ID + 128] = np.eye(128, dtype=np.float32)
    c[:, CS_ONES:CS_ONES + 128] = 1.0
    c[0:64, CS_BONES:CS_BONES + 64] = 1.0
    c[64:128, CS_BONES + 64:CS_BONES + 128] = 1.0
    s = np.arange(128)[:, None]
    t64 = np.arange(64)[None, :]
    c[:, CS_M64F:CS_M64F + 64] = ((s % 64) <= t64)
    c[:, CS_M64B:CS_M64B + 64] = ((s % 64) >= t64)
    t = np.arange(128)[None, :]
    c[:, CS_M128F:CS_M128F + 128] = (s <= t) & ((s // 32) == (t // 32))
    c[:, CS_M128B:CS_M128B + 128] = (s >= t) & ((s // 32) == (t // 32))
    c[:, CS_RWF:CS_RWF + 128] = (s < t)
    c[:, CS_RWF + 128:CS_RWF + 256] = (s <= t)
    c[:, CS_RWB:CS_RWB + 128] = (s > t)
    c[:, CS_RWB + 128:CS_RWB + 256] = (s >= t)
    for j in range(16):
        p = j % 8
        d, h = p // 4, p % 4
        c[(2 * d + 1) * 4 + h, CS_SELF + j] = 1.0
        if j >= 8:
            c[(2 * d) * 4 + h, CS_SELI + j] = 1.0
    c[0:8, CS_SGN] = 1.0
    c[8:16, CS_SGN] = -1.0
    for j in range(16):
        if (j % 8) < 4:
            c[j, CS_MDF] = 1.0
        else:
            c[j, CS_MDB] = 1.0
    return c


def _pack_layer(l, P):
    pc = np.zeros((128, NPC), np.float32)
    for i in range(4):
        pc[:, PC_NW + 8 * i:PC_NW + 8 * i + 8] = _cols(P['norms'][l, i], 8)
    pc[:, PC_BADA:PC_BADA + 48] = _cols(P['b_ada'][l], 48)
    pc[:, PC_LB0:PC_LB0 + 4] = _cols(P['hgrn_lb'][0], 4)
    pc[:, PC_LB1:PC_LB1 + 4] = _cols(P['hgrn_lb'][1], 4)
    pc[:, PC_HNORM:PC_HNORM + 4] = _cols(P['hgrn_norm'][l], 4)
    for j in range(4):
        pc[:, PC_MU + 15 * j:PC_MU + 15 * j + 15] = _cols(P['rwkv_mu'][l, j], 15)
    for d in range(2):
        pc[:, PC_W0 + 4 * d:PC_W0 + 4 * d + 4] = _cols(P['rwkv_w0'][l, d], 4)
        pc[:, PC_A0 + 4 * d:PC_A0 + 4 * d + 4] = _cols(P['rwkv_a0'][l, d], 4)
    pc[:, PC_KK:PC_KK + 4] = _cols(P['rwkv_kk'][l], 4)
    pc[:, PC_KA:PC_KA + 4] = _cols(P['rwkv_ka'][l], 4)
    pc[:, PC_RK:PC_RK + 4] = _cols(P['rwkv_rk'][l], 4)
    pc[:, PC_LN0:PC_LN0 + 4] = _cols(P['rwkv_ln'][l, 0], 4)
    pc[:, PC_LN1:PC_LN1 + 4] = _cols(P['rwkv_ln'][l, 1], 4)
    pc[:, PC_MNORM:PC_MNORM + 4] = _cols(P['mlstm_norm'][l], 4)
    gb = np.asarray(P['mlstm_gate_b'][l], np.float32)
    for j in range(16):
        p = j % 8
        d, h = p // 4, p % 4
        pc[j, PC_GBF] = gb[2 * d + 1, h]
        if j >= 8:
            pc[j, PC_GBI] = gb[2 * d, h]
    return pc


_NC_CACHE = {}


def kernel(**inp):
    P = {k: np.asarray(v) for k, v in inp.items()}
    key = tuple(sorted(DBG.items()))
    if key not in _NC_CACHE:
        prog = Prog()
        prog.build()
        _NC_CACHE[key] = prog
    prog = _NC_CACHE[key]
    nc = prog.nc
    cst = _build_consts()
    pc = np.stack([_pack_layer(l, P) for l in range(DEPTH)], 0)
    f32 = lambda a: np.ascontiguousarray(a, dtype=np.float32)
    shared = {"pc": pc, "cst": cst, "w_ada": f32(P['w_ada']), "w_in": f32(P['w_in']), "rwkv_w2": f32(P['rwkv_w2']),
              "rwkv_a2": f32(P['rwkv_a2']), "rwkv_g2": f32(P['rwkv_g2']), "w_branch": f32(P['w_branch']), "w_out": f32(P['w_out']),
              "w_ffn_in": f32(P['w_ffn_in']), "w_ffn_out": f32(P['w_ffn_out'])}
    in_maps = []
    for core in range(NCORES):
        m = dict(shared)
        cc = np.zeros((128, NCC), np.float32)
        if core < 4:
            b = core
            x = P['x_sample'][b]
            cond = P['c'][b]
            cc[:, CC_KEEP] = 1.0
            cc[:, CC_MUF:CC_MUF + 4] = 1.0
            m["hs0"] = f32(P['state_hgrn'][b])
            m["rs0"] = f32(np.swapaxes(P['state_rwkv'][b], -1, -2))
            m["cs0"] = f32(P['state_mlstm_C'][b])
            m["ns0"] = f32(P['state_mlstm_n'][b][..., None])
            cc[:, CC_M0:CC_M0 + 16] = P['state_mlstm_m'][b].reshape(1, 16)
        else:
            g = core - 4
            x = P['x_prompt'][g * 8:(g + 1) * 8].reshape(TOK, D)
            cond = P['c_ctx']
            cc[:, CC_MUF:CC_MUF + 2] = 1.0
            m["hs0"] = np.zeros((DEPTH, 2, 4, 128, 128), np.float32)
            m["rs0"] = np.zeros((DEPTH, 2, 8, 64, 64), np.float32)
            m["cs0"] = np.zeros((DEPTH, 2, 4, 128, 128), np.float32)
            m["ns0"] = np.zeros((DEPTH, 2, 4, 128, 1), np.float32)
            for r in range(1, 32):
                cc[:, CC_BML + r - 1] = 1.0 if (r * 64) % 256 != 0 else 0.0
            for r in range(0, 31):
                cc[:, CC_BMR + r] = 1.0 if ((r + 1) * 64) % 256 != 0 else 0.0
        m["xT"] = f32(x.T)
        cc[:, CC_COND:CC_COND + 8] = _cols(cond, 8)
        m["cc"] = cc
        in_maps.append(m)
    res = run_bass_kernel_spmd(nc, in_maps, core_ids=list(range(NCORES)))
    R = res.results
    kernel.last_results = R
    y_sample = np.stack([R[c]["yT"].T for c in range(4)], 0)
    y_prompt = np.concatenate([R[c]["yT"].T.reshape(8, 256, D) for c in range(4, 8)], 0)

    def gather(name, tail):
        outs = []
        for c in range(4, 8):
            a = R[c][name]
            outs.append(np.moveaxis(a, 2, 0))
        return np.concatenate(outs, 0)
    new_h = gather("so_h", None)
    new_r = np.swapaxes(gather("so_r", None), -1, -2)
    new_C = gather("so_c", None)
    new_n = gather("so_n", None)[..., 0]
    mm_ = []
    for c in range(4, 8):
        a = R[c]["so_m"]
        mm_.append(np.transpose(a[:, 8:16, :].reshape(DEPTH, 2, 4, NSEG), (3, 0, 1, 2)))
    new_m = np.concatenate(mm_, 0)
    return (f32(y_prompt), f32(y_sample), f32(new_h), f32(new_r), f32(new_C), f32(new_n), f32(new_m))
```
